# Optimizing a Trainium2 kernel written in Bass

```python
import math
import jax, jax.numpy as jnp
from jax import lax
import numpy as np

D_MODEL = 1024
BATCH = 16
SEQ = 4096
DEPTH = 1

MIX_WIDTH = D_MODEL
ATTN_WIDTH = MIX_WIDTH // 2
HEAD_DIM = 64
N_HEADS = ATTN_WIDTH // HEAD_DIM
SSM_WIDTH = MIX_WIDTH - ATTN_WIDTH
SSM_GROUP = 16
N_SSM_GROUPS = SSM_WIDTH // SSM_GROUP
STATE_DIM = 64
Q_BLOCK = 128
D_FF = ((8 * D_MODEL // 3 + 127) // 128) * 128
CONV_WIDTH = 3
IN_COLS = 3 * ATTN_WIDTH + N_HEADS + SSM_WIDTH
N_MOD = 6
EPS = 1e-6
NEG_INF = -1e30

kernel_name = "fox_s5_hymba_convffn_adaln"


def rmsnorm(x, g):
    x32 = x.astype(jnp.float32)
    y = x32 * lax.rsqrt(jnp.mean(x32 * x32, axis=-1, keepdims=True) + EPS)
    return (y * g.astype(jnp.float32)).astype(x.dtype)


def forgetting_attention(q, k, v, log_f):
    bsz, seq, nh, dh = q.shape
    nb = seq // Q_BLOCK
    cum = jnp.cumsum(log_f, axis=1).transpose(0, 2, 1)
    q_blocks = q.reshape(bsz, nb, Q_BLOCK, nh, dh).transpose(1, 0, 2, 3, 4)
    cum_blocks = cum.reshape(bsz, nh, nb, Q_BLOCK).transpose(2, 0, 1, 3)
    kpos = jnp.arange(seq)
    scale = dh ** -0.5

    def one_block(args):
        q_i, cum_i, i = args
        qpos = i * Q_BLOCK + jnp.arange(Q_BLOCK)
        s = jnp.einsum('bqhd,bkhd->bhqk', q_i, k,
                       preferred_element_type=jnp.float32) * scale
        s = s + cum_i[..., :, None] - cum[:, :, None, :]
        mask = kpos[None, :] <= qpos[:, None]
        s = jnp.where(mask, s, NEG_INF)
        p = jax.nn.softmax(s, axis=-1)
        return jnp.einsum('bhqk,bkhd->bqhd', p.astype(v.dtype), v)

    out = lax.map(one_block, (q_blocks, cum_blocks, jnp.arange(nb)))
    return out.transpose(1, 0, 2, 3, 4).reshape(bsz, seq, nh * dh)


def _scan_op(e1, e2):
    a1, b1 = e1
    a2, b2 = e2
    return a1 * a2, a2 * b1 + b2


def s5_ssm(u, a_re, a_im, log_dt, b_re, b_im, c_re, c_im, d_skip, w_glu, b_glu):
    bsz, seq, _ = u.shape
    f32 = jnp.float32
    ug = u.astype(f32).reshape(bsz, seq, N_SSM_GROUPS, SSM_GROUP)
    lam = lax.complex(a_re.astype(f32), a_im.astype(f32))
    dt = jnp.exp(log_dt.astype(f32))[:, None]
    lam_bar = jnp.exp(lam * dt)
    b_mat = lax.complex(b_re.astype(f32), b_im.astype(f32))
    b_bar = ((lam_bar - 1.0) / lam)[:, :, None] * b_mat
    bu = jnp.einsum('bsgc,gpc->bsgp', ug.astype(jnp.complex64), b_bar)
    a = jnp.broadcast_to(lam_bar, (1, seq) + lam_bar.shape)
    _, h = lax.associative_scan(_scan_op, (a, bu), axis=1)
    c_mat = lax.complex(c_re.astype(f32), c_im.astype(f32))
    y = jnp.real(jnp.einsum('bsgp,gcp->bsgc', h, c_mat)) + d_skip.astype(f32) * ug
    z = jax.nn.gelu(y)
    gate = jnp.einsum('bsgc,gcd->bsgd', z, w_glu.astype(f32)) + b_glu.astype(f32)
    out = z * jax.nn.sigmoid(gate)
    return out.reshape(bsz, seq, SSM_WIDTH).astype(u.dtype)


def causal_dwconv(h, w, b):
    ch = h.shape[-1]
    y = lax.conv_general_dilated(h, w.astype(h.dtype)[:, None, :], window_strides=(1,),
                                 padding=[(CONV_WIDTH - 1, 0)],
                                 dimension_numbers=('NWC', 'WIO', 'NWC'),
                                 feature_group_count=ch)
    return y + b.astype(h.dtype)


def setup_inputs(seed: int = 0) -> dict:
    key = jax.random.key(seed)
    ks = jax.random.split(key, 26)
    f32 = jnp.float32
    nrm = lambda k, shp, s: jax.random.normal(k, shp, f32) * s
    L, D, G, P, C = DEPTH, D_MODEL, N_SSM_GROUPS, STATE_DIM, SSM_GROUP
    a_im_base = math.pi * jnp.arange(P, dtype=f32)
    return {
        "x": nrm(ks[0], (BATCH, SEQ, D), 1.0),
        "c": nrm(ks[1], (BATCH, D), 1.0),
        "w_ada": nrm(ks[2], (L, D, N_MOD * D), 0.5 * D ** -0.5),
        "b_ada": nrm(ks[3], (L, N_MOD * D), 0.02),
        "g_mix": 1.0 + nrm(ks[4], (L, D), 0.02),
        "w_in": nrm(ks[5], (L, D, IN_COLS), D ** -0.5),
        "b_fgate": 3.0 + nrm(ks[6], (L, N_HEADS), 0.5),
        "a_re": -0.5 + nrm(ks[7], (L, G, P), 0.01),
        "a_im": a_im_base + nrm(ks[8], (L, G, P), 0.01),
        "log_dt": jax.random.uniform(ks[9], (L, G), f32, math.log(1e-3), math.log(1e-1)),
        "ssm_b_re": nrm(ks[10], (L, G, P, C), (2 * C) ** -0.5),
        "ssm_b_im": nrm(ks[11], (L, G, P, C), (2 * C) ** -0.5),
        "ssm_c_re": nrm(ks[12], (L, G, C, P), (2 * P) ** -0.5),
        "ssm_c_im": nrm(ks[13], (L, G, C, P), (2 * P) ** -0.5),
        "d_skip": nrm(ks[14], (L, G, C), 1.0),
        "w_glu": nrm(ks[15], (L, G, C, C), C ** -0.5),
        "b_glu": nrm(ks[16], (L, G, C), 0.02),
        "g_attn_out": 1.0 + nrm(ks[17], (L, ATTN_WIDTH), 0.02),
        "g_ssm_out": 1.0 + nrm(ks[18], (L, SSM_WIDTH), 0.02),
        "w_out": nrm(ks[19], (L, MIX_WIDTH, D), MIX_WIDTH ** -0.5),
        "g_ffn": 1.0 + nrm(ks[20], (L, D), 0.02),
        "w_up": nrm(ks[21], (L, D, 2 * D_FF), D ** -0.5),
        "conv_w": nrm(ks[22], (L, CONV_WIDTH, D_FF), CONV_WIDTH ** -0.5),
        "conv_b": nrm(ks[23], (L, D_FF), 0.02),
        "w_down": nrm(ks[24], (L, D_FF, D), D_FF ** -0.5),
        "g_final": 1.0 + nrm(ks[25], (D,), 0.02),
    }


def reference(x, c, w_ada, b_ada, g_mix, w_in, b_fgate, a_re, a_im, log_dt,
              ssm_b_re, ssm_b_im, ssm_c_re, ssm_c_im, d_skip, w_glu, b_glu,
              g_attn_out, g_ssm_out, w_out, g_ffn, w_up, conv_w, conv_b, w_down,
              g_final):
    bsz, seq, _ = x.shape
    silu_c = jax.nn.silu(c)
    for l in range(DEPTH):
        mod = (silu_c @ w_ada[l] + b_ada[l])[:, None, :]
        sh_m, sc_m, gt_m, sh_f, sc_f, gt_f = jnp.split(mod, N_MOD, axis=-1)

        h = rmsnorm(x, g_mix[l]) * (1.0 + sc_m) + sh_m
        proj = h @ w_in[l]
        q, k, v, f_logit, u = jnp.split(
            proj, [ATTN_WIDTH, 2 * ATTN_WIDTH, 3 * ATTN_WIDTH, 3 * ATTN_WIDTH + N_HEADS], axis=-1)
        q = q.reshape(bsz, seq, N_HEADS, HEAD_DIM)
        k = k.reshape(bsz, seq, N_HEADS, HEAD_DIM)
        v = v.reshape(bsz, seq, N_HEADS, HEAD_DIM)
        log_f = jax.nn.log_sigmoid(f_logit.astype(jnp.float32) + b_fgate[l].astype(jnp.float32))
        attn = rmsnorm(forgetting_attention(q, k, v, log_f), g_attn_out[l])
        ssm = rmsnorm(s5_ssm(u, a_re[l], a_im[l], log_dt[l], ssm_b_re[l], ssm_b_im[l],
                             ssm_c_re[l], ssm_c_im[l], d_skip[l], w_glu[l], b_glu[l]),
                      g_ssm_out[l])
        mix = jnp.concatenate([attn, ssm], axis=-1) @ w_out[l]
        x = x + gt_m * mix

        h = rmsnorm(x, g_ffn[l]) * (1.0 + sc_f) + sh_f
        gate_pre, val = jnp.split(h @ w_up[l], 2, axis=-1)
        gate_pre = causal_dwconv(gate_pre, conv_w[l], conv_b[l])
        y = (jax.nn.silu(gate_pre) * val) @ w_down[l]
        x = x + gt_f * y
    return rmsnorm(x, g_final)
```

```python
import numpy as np
from contextlib import ExitStack
import concourse.bass as bass
import concourse.mybir as mybir
from concourse.bass_utils import run_bass_kernel_spmd

F32 = mybir.dt.float32
BF16 = mybir.dt.bfloat16
AF = mybir.ActivationFunctionType
ALU = mybir.AluOpType

D = 1024
SEQ = 4096
NB = 2
NTOK = NB * SEQ
TB = 512
NBLK = NTOK // TB
BPS = SEQ // TB
DFF = 2816
NFC = DFF // 128
INC = 2056
EPS = 1e-6
AW = 53000


class Buf:
    __slots__ = ("name", "wset", "readers")

    def __init__(self, name):
        self.name = name
        self.wset = []
        self.readers = []


class Op:
    __slots__ = ("id", "eng", "fn", "deps", "raw", "needs_inc", "tick", "dma_key", "dma_val")


class Sched:
    ENGS = ("pe", "act", "dve", "pool", "sp")

    def __init__(self, nc):
        self.nc = nc
        self.ops = []
        self.per_eng = {e: [] for e in self.ENGS}
        self.dma_cnt = {}

    def op(self, eng, fn, reads=(), writes=(), dma_key=None):
        o = Op()
        o.id = len(self.ops)
        o.eng = eng
        o.fn = fn
        o.needs_inc = False
        o.tick = None
        o.dma_key = dma_key
        o.dma_val = None
        deps = set()
        raw = set()
        for b in reads:
            deps.update(b.wset)
            raw.update(b.wset)
        for b in writes:
            deps.update(b.wset)
            deps.update(b.readers)
        for b in reads:
            b.readers.append(o.id)
        for b in writes:
            if b.readers:
                b.wset = [o.id]
                b.readers = []
            else:
                b.wset.append(o.id)
                if len(b.wset) > 6:
                    b.wset = b.wset[-6:]
        deps.discard(o.id)
        raw.discard(o.id)
        o.deps = deps
        o.raw = raw
        if dma_key is not None:
            self.dma_cnt[dma_key] = self.dma_cnt.get(dma_key, 0) + 1
            o.dma_val = 16 * self.dma_cnt[dma_key]
        self.ops.append(o)
        self.per_eng[eng].append(o)
        return o

    def fence(self):
        last = []
        for e in self.ENGS:
            for o in reversed(self.per_eng[e]):
                if o.dma_key is None and o.fn is not None:
                    last.append(o.id)
                    break
        lastd = {}
        for o in self.ops:
            if o.dma_key is not None:
                lastd[o.dma_key] = o.id
        deps = set(last) | set(lastd.values())
        for e in self.ENGS:
            o = Op()
            o.id = len(self.ops)
            o.eng = e
            o.fn = None
            o.deps = set(deps)
            o.raw = set(deps)
            o.needs_inc = False
            o.tick = None
            o.dma_key = None
            o.dma_val = None
            self.ops.append(o)
            self.per_eng[e].append(o)

    def emit(self):
        nc = self.nc
        ops = self.ops
        for o in ops:
            for d in o.deps:
                dd = ops[d]
                if dd.dma_key is None:
                    if dd.eng == o.eng and o.dma_key is None and d not in o.raw:
                        continue
                    dd.needs_inc = True
        ticks = {e: 0 for e in self.ENGS}
        for e in self.ENGS:
            for o in self.per_eng[e]:
                if o.dma_key is None and o.needs_inc:
                    ticks[e] += 1
                    o.tick = ticks[e]
        with ExitStack() as es:
            esem = {e: es.enter_context(nc.semaphore("s_" + e)) for e in self.ENGS}
            dsem = {k: es.enter_context(nc.semaphore("d_" + str(k))) for k in self.dma_cnt}
            block = es.enter_context(nc.Block())

            def run(ename, eng):
                seen = {}
                for o in self.per_eng[ename]:
                    need = {}
                    for d in o.deps:
                        dd = ops[d]
                        if dd.dma_key is not None:
                            key = ("d", dd.dma_key)
                            val = dd.dma_val
                        else:
                            if dd.eng == ename and o.dma_key is None and d not in o.raw:
                                continue
                            key = ("e", dd.eng)
                            val = dd.tick
                        if seen.get(key, 0) >= val:
                            continue
                        if need.get(key, 0) < val:
                            need[key] = val
                    for key, val in need.items():
                        sem = dsem[key[1]] if key[0] == "d" else esem[key[1]]
                        eng.wait_ge(sem, val)
                        seen[key] = val
                    if o.fn is None:
                        continue
                    inst = o.fn(eng)
                    if o.dma_key is not None:
                        inst.then_inc(dsem[o.dma_key], 16)
                    elif o.needs_inc:
                        inst.then_inc(esem[ename], 1)
                if ename == "sp":
                    for k, c in self.dma_cnt.items():
                        if seen.get(("d", k), 0) < 16 * c:
                            eng.wait_ge(dsem[k], 16 * c)

            @block.tensor
            def _(e):
                run("pe", e)

            @block.scalar
            def _(e):
                run("act", e)

            @block.vector
            def _(e):
                run("dve", e)

            @block.gpsimd
            def _(e):
                run("pool", e)

            @block.sync
            def _(e):
                run("sp", e)


class Arena:
    def __init__(self, ap_f32):
        self.ap = ap_f32
        self.W = ap_f32.shape[1]
        self.off = 0

    def alloc(self, shape, dtype, parts=128):
        if isinstance(shape, int):
            shape = (shape,)
        n = int(np.prod(shape))
        esz = 4 if dtype == F32 else 2
        words = (n * esz + 3) // 4
        words = (words + 7) // 8 * 8
        assert self.off + words <= self.W, f"arena overflow {self.off}+{words}>{self.W}"
        a = self.ap[0:parts, self.off:self.off + words]
        self.off += words
        if dtype != F32:
            a = a.bitcast(dtype)
        a = a[:, 0:n]
        if len(shape) > 1:
            names = " ".join(f"d{i}" for i in range(len(shape)))
            kw = {f"d{i}": int(s) for i, s in enumerate(shape)}
            a = a.rearrange(f"p ({names}) -> p {names}", **kw)
        return a


def build(phases="SAB", dbg=()):
    nc = bass.Bass("TRN2", target_bir_lowering=False)

    def din(name, shape, dt=F32):
        return nc.dram_tensor(name, list(shape), dt, kind="ExternalInput").ap()

    x_d = din("x", [NTOK, D])
    c_d = din("c", [NB, D])
    wada_d = din("w_ada", [D, 6 * D])
    bada_d = din("b_ada2", [NB, 6 * D])
    win_d = din("w_in", [D, INC])
    wout_d = din("w_out", [D, D])
    wup_d = din("w_up", [D, 2 * DFF])
    wdn_d = din("w_down", [DFF, D])
    pvec_d = din("pvec", [128, 128])
    gfin_d = din("g_final", [1, D])
    bfg_d = din("b_fgate", [8, 1])
    sel_d = din("sel", [NB, NB * 128])
    ssm16_d = din("ssm16", [128, 3 * 16])
    ssmB_d = din("ssmB", [128, 2 * 256])
    ssmC_d = din("ssmC", [128, 2 * 256])
    wglu_d = din("wglu_blk", [128, 4 * 128])
    y_d = nc.dram_tensor("y", [NTOK, D], F32, kind="ExternalOutput").ap()
    x1_d = nc.dram_tensor("x1s", [NTOK, D], F32).ap()
    ssmT_d = nc.dram_tensor("ssmT", [4, 128, NTOK], BF16).ap()
    winb_d = nc.dram_tensor("win_b", [D, INC], BF16).ap()
    woutb_d = nc.dram_tensor("wout_b", [D, D], BF16).ap()
    wupb_d = nc.dram_tensor("wup_b", [D, 2 * DFF], BF16).ap()
    wdnb_d = nc.dram_tensor("wdn_b", [DFF, D], BF16).ap()
    dbg_d = {}
    for name, shape in dbg:
        dbg_d[name] = nc.dram_tensor(name, list(shape), F32, kind="ExternalOutput").ap()

    with ExitStack() as es:
        arena_t = es.enter_context(nc.sbuf_tensor("arena", [128, AW], F32))
        ps = [es.enter_context(nc.psum_tensor(f"ps{i}", [128, 512], F32)) for i in range(8)]
        PB = [Buf(f"ps{i}") for i in range(8)]
        S = Sched(nc)
        ar = Arena(arena_t[:])

        def dma(q, out, in_, key, reads=(), writes=()):
            S.op(q, lambda e: e.dma_start(out=out, in_=in_), reads=reads, writes=writes, dma_key=key)

        ident_f = ar.alloc((128,), F32)
        ident_b = ar.alloc((128,), BF16)
        ones_f = ar.alloc((128,), F32)
        pvec = ar.alloc((128,), F32)
        modT = ar.alloc((48, NB), F32)
        Am = ar.alloc((8, NB), F32)
        Af = ar.alloc((8, NB), F32)
        Gfin = ar.alloc((D,), F32)
        B_const = Buf("const")
        B_x1s = Buf("x1s")
        B_ssmT = Buf("ssmT")
        B_mod = Buf("mod")
        const_end = ar.off

        PV_GMIX, PV_GFFN, PV_GCAT, PV_CW, PV_CB, PV_DSK, PV_BGLU = 0, 8, 16, 24, 90, 112, 116

        S.op("pool", lambda e: e.memset(ident_f, 0.0), writes=[B_const])
        S.op("pool", lambda e: e.affine_select(out=ident_f, in_=ident_f, compare_op=ALU.not_equal, fill=1.0,
                                               base=0, pattern=[[-1, 128]], channel_multiplier=1),
             reads=[B_const], writes=[B_const])
        S.op("pool", lambda e: e.tensor_copy(out=ident_b, in_=ident_f), reads=[B_const], writes=[B_const])
        S.op("pool", lambda e: e.memset(ones_f, 1.0), writes=[B_const])
        dma("sp", pvec, pvec_d, "c0", writes=[B_const])
        dma("sp", Gfin, gfin_d.partition_broadcast(128), "c2", writes=[B_const])

        B_wcv = Buf("wconv")

        def emit_wconv():
            for kc in range(8):
                r = slice(kc * 128, (kc + 1) * 128)
                dma("pool", winb_d[r, :], win_d[r, :], "wcv", writes=[B_wcv])
            for kc in range(0, 8, 2):
                r = slice(kc * 128, (kc + 2) * 128)
                dma("pool", woutb_d[r, :], wout_d[r, :], "wcv", writes=[B_wcv])
            for kc in range(8):
                r = slice(kc * 128, (kc + 1) * 128)
                dma("pool", wupb_d[r, :], wup_d[r, :], "wcv", writes=[B_wcv])
            for fc in range(0, NFC, 2):
                r = slice(fc * 128, (fc + 2) * 128)
                dma("pool", wdnb_d[r, :], wdn_d[r, :], "wcv", writes=[B_wcv])

        setup_off = ar.off
        c_sb = ar.alloc((D,), F32, parts=NB)
        sc_sb = ar.alloc((D,), F32, parts=NB)
        scT = ar.alloc((8, NB), F32)
        mod_tm = ar.alloc((6 * D,), F32, parts=NB)
        bada = ar.alloc((6 * D,), F32, parts=NB)
        wa = [ar.alloc((8, 512), F32) for _ in range(2)]
        B_c, B_sc, B_scT, B_modtm, B_bada = Buf("c"), Buf("sc"), Buf("scT"), Buf("modtm"), Buf("bada")
        B_wa = [Buf("wa0"), Buf("wa1")]
        dma("sp", c_sb, c_d, "c3", writes=[B_c])
        dma("act", bada, bada_d, "c4", writes=[B_bada])
        S.op("act", lambda e: e.activation(out=sc_sb, in_=c_sb, func=AF.Silu), reads=[B_c], writes=[B_sc])

        def f_scT(e):
            for kc in range(8):
                i = e.transpose(out=ps[0][:, kc * NB:(kc + 1) * NB], in_=sc_sb[0:NB, kc * 128:(kc + 1) * 128],
                                identity=ident_f[0:NB, 0:NB])
            return i
        S.op("pe", f_scT, reads=[B_sc, B_const], writes=[PB[0]])
        S.op("dve", lambda e: e.tensor_copy(out=scT.rearrange("p a b -> p (a b)"), in_=ps[0][:, 0:8 * NB]),
             writes=[B_scT, PB[0]])
        wada_v = wada_d.rearrange("(kc p) n -> p kc n", p=128)
        for cb in range(12):
            w = wa[cb % 2]
            dma("sp" if cb % 2 == 0 else "act", w, wada_v[:, :, cb * 512:(cb + 1) * 512], f"wa{cb % 2}",
                writes=[B_wa[cb % 2]])
            pb = 1 + cb % 2

            def f_mod(e, w=w, pb=pb):
                for kc in range(8):
                    i = e.matmul(ps[pb][0:NB, :], lhsT=scT[:, kc, :], rhs=w[:, kc, :], start=(kc == 0), stop=(kc == 7))
                return i
            S.op("pe", f_mod, reads=[B_scT, B_wa[cb % 2]], writes=[PB[pb]])
            S.op("dve", lambda e, cb=cb, pb=pb: e.tensor_tensor(out=mod_tm[:, cb * 512:(cb + 1) * 512], in0=ps[pb][0:NB, :],
                                                               in1=bada[:, cb * 512:(cb + 1) * 512], op=ALU.add),
                 reads=[B_bada], writes=[B_modtm, PB[pb]])

        def f_modT(e):
            for j in range(48):
                i = e.transpose(out=ps[3][:, j * NB:(j + 1) * NB], in_=mod_tm[0:NB, j * 128:(j + 1) * 128],
                                identity=ident_f[0:NB, 0:NB])
            return i
        S.op("pe", f_modT, reads=[B_modtm, B_const], writes=[PB[3]])
        S.op("dve", lambda e: e.tensor_copy(out=modT.rearrange("p a b -> p (a b)"), in_=ps[3][:, 0:48 * NB]),
             writes=[B_mod, PB[3]])
        for (A_t, gcol, scj) in ((Am, PV_GMIX, 8), (Af, PV_GFFN, 32)):
            for b in range(NB):
                S.op("dve", lambda e, A_t=A_t, gcol=gcol, scj=scj, b=b: e.scalar_tensor_tensor(
                    out=A_t[:, :, b], in0=modT[:, scj:scj + 8, b], scalar=1.0, in1=pvec[:, gcol:gcol + 8],
                    op0=ALU.add, op1=ALU.mult), reads=[B_mod, B_const], writes=[B_mod])
        if "S" not in phases:
            emit_wconv()
        S.fence()
        ar.off = const_end

        def build_G(G_t, B_G, j0, b, dgs, B_dgs):
            for hf in range(2):
                for k4 in range(4):
                    kc = hf * 4 + k4
                    S.op("dve", lambda e, kc=kc, k4=k4: e.tensor_scalar(out=dgs[k4 % 2], in0=ident_f,
                                                                         scalar1=modT[:, j0 + kc, b:b + 1], scalar2=None,
                                                                         op0=ALU.mult),
                         reads=[B_mod, B_const], writes=[B_dgs[k4 % 2]])
                    S.op("pe", lambda e, k4=k4: e.matmul(ps[7][:, k4 * 128:(k4 + 1) * 128], lhsT=ones_f, rhs=dgs[k4 % 2],
                                                          start=True, stop=True),
                         reads=[B_dgs[k4 % 2], B_const], writes=[PB[7]])
                S.op("act", lambda e, hf=hf: e.copy(out=G_t[:, hf * 512:(hf + 1) * 512], in_=ps[7][:, :]),
                     writes=[B_G, PB[7]])


        def make_front(src_d, A_t, shj, tag, nxs=1, own_junk=False):
            xa = [ar.alloc((D,), F32) for _ in range(2)]
            xs = [ar.alloc((D,), BF16) for _ in range(nxs)]
            hT = ar.alloc((8, TB), BF16)
            ssq = ar.alloc((8,), F32)
            B_xa = [Buf(tag + "xa0"), Buf(tag + "xa1")]
            B_xs = [Buf(tag + "xs%d" % i) for i in range(nxs)]
            B_hTe, B_hTo = Buf(tag + "hTe"), Buf(tag + "hTo")
            B_st = [Buf(tag + "st%d" % i) for i in range(4)]
            if own_junk:
                junk = ar.alloc((D,), BF16)
                B_junk = Buf(tag + "junk")
            pv0 = ps[0][:, :].bitcast(BF16).rearrange("p (kc t) -> p kc t", kc=4)
            pv1 = ps[1][:, :].bitcast(BF16).rearrange("p (kc t) -> p kc t", kc=4)

            class F:
                pass

            def ld(tb, t):
                k = t % 2
                r0 = tb * TB + t * 128
                dma("sp", xa[k], src_d[r0:r0 + 128, :], tag + f"xa{k}", writes=[B_xa[k]])

            def st(tb, t):
                k = t % 2
                kx = t % nxs
                jo = junk if own_junk else xs[kx]
                jb_ = B_junk if own_junk else B_xs[kx]
                S.op("act", lambda e, k=k, t=t, jo=jo: e.activation(out=jo, in_=xa[k], func=AF.Square,
                                                                     accum_out=ssq[:, t:t + 1]),
                     reads=[B_xa[k]], writes=[jb_, B_st[t]])
                S.op("act", lambda e, t=t: e.activation(out=ssq[:, 4 + t:5 + t], in_=ssq[:, t:t + 1], func=AF.Ln,
                                                         scale=1.0 / D, bias=pvec[:, 127:128]),
                     reads=[B_st[t], B_const], writes=[B_st[t]])
                S.op("act", lambda e, t=t: e.activation(out=ssq[:, t:t + 1], in_=ssq[:, 4 + t:5 + t], func=AF.Exp,
                                                         scale=-0.5),
                     reads=[B_st[t]], writes=[B_st[t]])

            def xsop(tb, t):
                k = t % 2
                kx = t % nxs
                S.op("dve", lambda e, k=k, t=t, kx=kx: e.tensor_scalar(out=xs[kx], in0=xa[k], scalar1=ssq[:, t:t + 1],
                                                                       scalar2=None, op0=ALU.mult),
                     reads=[B_xa[k], B_st[t]], writes=[B_xs[kx]])

            def tr(tb, t):
                t2 = t % 2
                kx = t % nxs

                def f_tr(e, kx=kx, t2=t2):
                    for kc in range(8):
                        pv = pv0 if kc < 4 else pv1
                        i = e.transpose(out=pv[:, kc % 4, t2 * 128:(t2 + 1) * 128],
                                        in_=xs[kx][:, kc * 128:(kc + 1) * 128], identity=ident_b)
                    return i
                S.op("pe", f_tr, reads=[B_xs[kx], B_const], writes=[PB[0], PB[1]])

            def evac(tb, half):
                b = tb // BPS
                for kc in range(4):
                    S.op("act", lambda e, kc=kc, b=b, half=half: e.activation(
                        out=hT[:, kc, half * 256:(half + 1) * 256], in_=pv0[:, kc, :], func=AF.Identity,
                        scale=A_t[:, kc, b:b + 1], bias=modT[:, shj + kc, b:b + 1]),
                        reads=[B_mod], writes=[B_hTe, PB[0]])
                for kc in range(4, 8):
                    S.op("dve", lambda e, kc=kc, b=b, half=half: e.tensor_scalar(
                        out=hT[:, kc, half * 256:(half + 1) * 256], in0=pv1[:, kc - 4, :],
                        scalar1=A_t[:, kc, b:b + 1], scalar2=modT[:, shj + kc, b:b + 1],
                        op0=ALU.mult, op1=ALU.add),
                        reads=[B_mod], writes=[B_hTo, PB[1]])

            def pre(tb, t):
                ld(tb, t)
                st(tb, t)
                xsop(tb, t)

            def full(tb):
                for t in range(4):
                    pre(tb, t)
                    tr(tb, t)
                    if t % 2 == 1:
                        evac(tb, t // 2)
            F.ld, F.st, F.xsop, F.tr, F.evac, F.pre, F.full = ld, st, xsop, tr, evac, pre, full
            return F, hT, [B_hTe, B_hTo]

        def phase_S():
            import math
            ar.off = const_end
            I32 = mybir.dt.int32
            wu = ar.alloc((8, 512), BF16)
            B_wu = Buf("wu")
            for kc in range(8):
                dma("pool", wu[:, kc, :], win_d[kc * 128:(kc + 1) * 128, 1544:2056], "wu", writes=[B_wu])
            FR, hT, B_hT = make_front(x_d, Am, 0, "S", nxs=2, own_junk=True)
            uT = ar.alloc((4, TB), BF16)
            LagW = ar.alloc((4, 8, 128), BF16)
            W1 = ar.alloc((4, 2, 8, 128), BF16)
            W2 = ar.alloc((16, 8, 2, 32), BF16)
            WgluB = ar.alloc((4, 128), BF16)
            Mp = [ar.alloc((16, 64), F32) for _ in range(2)]
            Mm = [ar.alloc((16, 64), F32) for _ in range(2)]
            scanmask = ar.alloc((16, 64), F32)
            mu = [ar.alloc((16,), F32) for _ in range(2)]
            Hc = [ar.alloc((16,), F32) for _ in range(2)]
            tc_ = [ar.alloc((16,), F32) for _ in range(4)]
            tmp = [ar.alloc((16, 64), F32) for _ in range(2)]
            alias_start = ar.off
            Hinj = [ar.alloc((16, 64), F32) for _ in range(2)]
            Wm = [ar.alloc((16, 64), F32) for _ in range(2)]
            Zs = [ar.alloc((16, 64), F32) for _ in range(2)]
            Hb = [ar.alloc((16, 65), BF16) for _ in range(2)]
            yf = ar.alloc((4, TB), F32)
            Ylag = [ar.alloc((4, TB), F32) for _ in range(2)]
            zf2 = [ar.alloc((4, TB), F32) for _ in range(2)]
            zb = ar.alloc((4, TB), BF16)
            g1 = [ar.alloc((TB,), F32) for _ in range(2)]
            sg = [ar.alloc((TB,), F32) for _ in range(2)]
            sq = ar.alloc((4, TB), F32)
            rs = ar.alloc((TB,), F32)
            onb = ar.alloc((4, TB), BF16)
            run_end = ar.off
            print("phase S arena use (runtime)", ar.off)
            ar.off = alias_start
            s16 = ar.alloc((48,), F32)
            sB = ar.alloc((2, 16, 16), F32)
            sC = ar.alloc((2, 16, 16), F32)
            T = [ar.alloc((16,), F32) for _ in range(16)]
            Ti = ar.alloc((16,), F32)
            Pk = [ar.alloc((9, 16), F32) for _ in range(2)]
            gam = [ar.alloc((16,), F32) for _ in range(2)]
            Bb = [ar.alloc((16, 16), F32) for _ in range(2)]
            Xs = [ar.alloc((8, 16, 16), F32) for _ in range(2)]
            Xpad = [ar.alloc((8, 16, 2, 16), F32) for _ in range(2)]
            Cp = [ar.alloc((16, 2, 16), F32) for _ in range(2)]
            Et = [ar.alloc((16, 32), F32) for _ in range(2)]
            mask2 = ar.alloc((2,), F32)
            bmask = ar.alloc((128,), F32)
            mpow = [ar.alloc((16,), F32) for _ in range(2)]
            mpow2 = [ar.alloc((16,), F32) for _ in range(2)]
            nu = [ar.alloc((16,), F32) for _ in range(2)]
            print("phase S arena use (with setup)", ar.off)
            B_su = Buf("ssetup")

            def sop(eng, fn):
                S.op(eng, fn, reads=[B_su, B_const], writes=[B_su])

            def V(fn):
                sop("dve", fn)

            dma("sp", s16, ssm16_d, "c6", writes=[B_su])
            dma("sp", sB.rearrange("p a b c -> p (a b c)"), ssmB_d, "c7", writes=[B_su])
            dma("sp", sC.rearrange("p a b c -> p (a b c)"), ssmC_d, "c8", writes=[B_su])
            dma("pool", WgluB.rearrange("p a b -> p (a b)"), wglu_d, "c9", writes=[B_su])
            a_re, a_im, ldt = s16[:, 0:16], s16[:, 16:32], s16[:, 32:48]
            dt, ardt, th, er, us, uc, sn, cs, lbr, lbi, den, nr, t0, t1, t2, t3 = T
            sop("pool", lambda e: e.memset(mask2, 0.0))
            sop("pool", lambda e: e.memset(mask2[0:64, 0:1], 1.0))
            sop("pool", lambda e: e.memset(mask2[64:128, 1:2], 1.0))
            sop("pool", lambda e: e.memset(scanmask, 1.0))
            sop("pool", lambda e: e.memset(scanmask[:, :, 0:1], 0.0))
            bm3 = bmask.rearrange("p (a b) -> p a b", a=4)
            sop("pool", lambda e: e.memset(bmask, 0.0))
            sop("pool", lambda e: e.affine_select(out=bm3, in_=bm3, compare_op=ALU.is_gt, fill=1.0, base=1 - 32,
                                                  pattern=[[-32, 4], [0, 32]], channel_multiplier=1))
            sop("pool", lambda e: e.affine_select(out=bm3, in_=bm3, compare_op=ALU.is_ge, fill=0.0, base=0,
                                                  pattern=[[-32, 4], [0, 32]], channel_multiplier=1))
            sop("act", lambda e: e.activation(out=dt, in_=ldt, func=AF.Exp))
            V(lambda e: e.tensor_tensor(out=ardt, in0=a_re, in1=dt, op=ALU.mult))
            V(lambda e: e.tensor_tensor(out=th, in0=a_im, in1=dt, op=ALU.mult))
            sop("act", lambda e: e.activation(out=er, in_=ardt, func=AF.Exp))
            for (u_, shift) in ((us, 8.5), (uc, 8.75)):
                V(lambda e, u_=u_, shift=shift: e.tensor_scalar(out=u_, in0=th, scalar1=1.0 / (2 * math.pi), scalar2=shift,
                                                                op0=ALU.mult, op1=ALU.add))
                V(lambda e, u_=u_: e.tensor_copy(out=Ti.bitcast(I32), in_=u_))
                V(lambda e: e.tensor_copy(out=t0, in_=Ti.bitcast(I32)))
                V(lambda e, u_=u_: e.tensor_tensor(out=u_, in0=u_, in1=t0, op=ALU.subtract))
                V(lambda e, u_=u_: e.tensor_scalar(out=t0, in0=u_, scalar1=0.0, scalar2=None, op0=ALU.is_lt))
                V(lambda e, u_=u_: e.tensor_tensor(out=u_, in0=u_, in1=t0, op=ALU.add))
                V(lambda e, u_=u_: e.tensor_scalar(out=u_, in0=u_, scalar1=2 * math.pi, scalar2=math.pi,
                                                   op0=ALU.mult, op1=ALU.subtract))
            sop("act", lambda e: e.activation(out=sn, in_=us, func=AF.Sin))
            sop("act", lambda e: e.activation(out=cs, in_=uc, func=AF.Sin))
            V(lambda e: e.tensor_tensor(out=lbr, in0=er, in1=cs, op=ALU.mult))
            V(lambda e: e.tensor_tensor(out=lbi, in0=er, in1=sn, op=ALU.mult))
            V(lambda e: e.tensor_tensor(out=den, in0=a_re, in1=a_re, op=ALU.mult))
            V(lambda e: e.tensor_tensor(out=t0, in0=a_im, in1=a_im, op=ALU.mult))
            V(lambda e: e.tensor_tensor(out=den, in0=den, in1=t0, op=ALU.add))
            V(lambda e: e.reciprocal(out=den, in_=den))
            V(lambda e: e.tensor_scalar(out=nr, in0=lbr, scalar1=-1.0, scalar2=None, op0=ALU.add))
            V(lambda e: e.tensor_tensor(out=t0, in0=nr, in1=a_re, op=ALU.mult))
            V(lambda e: e.tensor_tensor(out=t1, in0=lbi, in1=a_im, op=ALU.mult))
            V(lambda e: e.tensor_tensor(out=t0, in0=t0, in1=t1, op=ALU.add))
            V(lambda e: e.tensor_tensor(out=gam[0], in0=t0, in1=den, op=ALU.mult))
            V(lambda e: e.tensor_tensor(out=t0, in0=lbi, in1=a_re, op=ALU.mult))
            V(lambda e: e.tensor_tensor(out=t1, in0=nr, in1=a_im, op=ALU.mult))
            V(lambda e: e.tensor_tensor(out=t0, in0=t0, in1=t1, op=ALU.subtract))
            V(lambda e: e.tensor_tensor(out=gam[1], in0=t0, in1=den, op=ALU.mult))

            def cmul(o_re, o_im, a_r, a_i, b_r, b_i, ta, tb_):
                V(lambda e: e.tensor_tensor(out=ta, in0=a_i, in1=b_i, op=ALU.mult))
                V(lambda e: e.tensor_tensor(out=o_re, in0=a_r, in1=b_r, op=ALU.mult))
                V(lambda e: e.tensor_tensor(out=o_re, in0=o_re, in1=ta, op=ALU.subtract))
                V(lambda e: e.tensor_tensor(out=tb_, in0=a_i, in1=b_r, op=ALU.mult))
                V(lambda e: e.tensor_tensor(out=o_im, in0=a_r, in1=b_i, op=ALU.mult))
                V(lambda e: e.tensor_tensor(out=o_im, in0=o_im, in1=tb_, op=ALU.add))

            V(lambda e: e.memset(Pk[0][:, 0, :], 1.0))
            V(lambda e: e.memset(Pk[1][:, 0, :], 0.0))
            for k in range(8):
                cmul(Pk[0][:, k + 1, :], Pk[1][:, k + 1, :], Pk[0][:, k, :], Pk[1][:, k, :], lbr, lbi, t2, t3)
            V(lambda e: e.tensor_copy(out=mu[0], in_=Pk[0][:, 8, :]))
            V(lambda e: e.tensor_copy(out=mu[1], in_=Pk[1][:, 8, :]))
            V(lambda e: e.tensor_tensor(out=t0, in0=mu[0], in1=mu[0], op=ALU.mult))
            V(lambda e: e.tensor_tensor(out=t1, in0=mu[1], in1=mu[1], op=ALU.mult))
            V(lambda e: e.tensor_tensor(out=t0, in0=t0, in1=t1, op=ALU.add))
            V(lambda e: e.reciprocal(out=t0, in_=t0))
            V(lambda e: e.tensor_tensor(out=nu[0], in0=mu[0], in1=t0, op=ALU.mult))
            V(lambda e: e.tensor_tensor(out=nu[1], in0=mu[1], in1=t0, op=ALU.mult))
            V(lambda e: e.tensor_scalar(out=nu[1], in0=nu[1], scalar1=-1.0, scalar2=None, op0=ALU.mult))
            for (Mt, base) in ((Mp, mu), (Mm, nu)):
                V(lambda e, Mt=Mt: e.memset(Mt[0][:, :, 0:1], 1.0))
                V(lambda e, Mt=Mt: e.memset(Mt[1][:, :, 0:1], 0.0))
                V(lambda e, base=base: e.tensor_copy(out=mpow[0], in_=base[0]))
                V(lambda e, base=base: e.tensor_copy(out=mpow[1], in_=base[1]))
                k = 1
                while k < 64:
                    br = mpow[0][:, :].unsqueeze(2).to_broadcast([128, 16, k])
                    bi = mpow[1][:, :].unsqueeze(2).to_broadcast([128, 16, k])
                    cmul(Mt[0][:, :, k:2 * k], Mt[1][:, :, k:2 * k], Mt[0][:, :, 0:k], Mt[1][:, :, 0:k], br, bi,
                         tmp[0][:, :, 0:k], tmp[1][:, :, 0:k])
                    if 2 * k < 64:
                        cmul(mpow2[0], mpow2[1], mpow[0], mpow[1], mpow[0], mpow[1], t2, t3)
                        V(lambda e: e.tensor_copy(out=mpow[0], in_=mpow2[0]))
                        V(lambda e: e.tensor_copy(out=mpow[1], in_=mpow2[1]))
                    k *= 2
            gbr = gam[0][:, :].unsqueeze(2).to_broadcast([128, 16, 16])
            gbi = gam[1][:, :].unsqueeze(2).to_broadcast([128, 16, 16])
            cmul(Bb[0], Bb[1], sB[:, 0], sB[:, 1], gbr, gbi, tmp[0][:, :, 0:16], tmp[1][:, :, 0:16])
            for tau in range(8):
                pr = Pk[0][:, tau, :].unsqueeze(2).to_broadcast([128, 16, 16])
                pi_ = Pk[1][:, tau, :].unsqueeze(2).to_broadcast([128, 16, 16])
                cmul(Xs[0][:, tau], Xs[1][:, tau], Bb[0], Bb[1], pr, pi_, tmp[0][:, :, 0:16], tmp[1][:, :, 0:16])
            for ri in range(2):
                for g2 in range(2):
                    V(lambda e, ri=ri, g2=g2: e.tensor_scalar(
                        out=Xpad[ri][:, :, :, g2, :].rearrange("p a b c -> p (a b) c"),
                        in0=Xs[ri].rearrange("p a b c -> p (a b) c"), scalar1=mask2[:, g2:g2 + 1], scalar2=None,
                        op0=ALU.mult))
                    V(lambda e, ri=ri, g2=g2: e.tensor_scalar(
                        out=Cp[ri][:, :, g2, :], in0=sC[:, ri], scalar1=mask2[:, g2:g2 + 1],
                        scalar2=(1.0 if ri == 0 else -1.0), op0=ALU.mult, op1=ALU.mult))
            for q in range(4):
                for tau in range(8):
                    def f_lag(e, q=q, tau=tau):
                        e.matmul(ps[2][:, 0:128], lhsT=Xpad[0][:, tau, 4 * q:4 * q + 4].rearrange("p a b c -> p (a b c)"),
                                 rhs=Cp[0][:, 4 * q:4 * q + 4].rearrange("p a b c -> p (a b c)"), start=True, stop=False)
                        return e.matmul(ps[2][:, 0:128], lhsT=Xpad[1][:, tau, 4 * q:4 * q + 4].rearrange("p a b c -> p (a b c)"),
                                        rhs=Cp[1][:, 4 * q:4 * q + 4].rearrange("p a b c -> p (a b c)"), start=False, stop=True)
                    S.op("pe", f_lag, reads=[B_su], writes=[PB[2]])
                    if tau == 0:
                        S.op("dve", lambda e: e.tensor_tensor(out=tmp[0].rearrange("p a b -> p (a b)")[:, 0:128], in0=ps[2][:, 0:128],
                                                              in1=bmask, op=ALU.mult), reads=[B_su], writes=[B_su, PB[2]])
                        V(lambda e, q=q: e.scalar_tensor_tensor(out=LagW[:, q, 0, :], in0=ident_f,
                                                                scalar=pvec[:, PV_DSK + q:PV_DSK + q + 1],
                                                                in1=tmp[0].rearrange("p a b -> p (a b)")[:, 0:128],
                                                                op0=ALU.mult, op1=ALU.add))
                    else:
                        S.op("dve", lambda e, q=q, tau=tau: e.tensor_tensor(out=LagW[:, q, tau, :], in0=ps[2][:, 0:128], in1=bmask,
                                                                            op=ALU.mult), reads=[B_su], writes=[B_su, PB[2]])
            for q in range(4):
                for ri in range(2):
                    for s in range(8):
                        S.op("pe", lambda e, q=q, ri=ri, s=s: e.transpose(
                            out=ps[3][:, 0:128], in_=Xpad[ri][:, 7 - s, 4 * q:4 * q + 4].rearrange("p a b c -> p (a b c)"),
                            identity=ident_f), reads=[B_su, B_const], writes=[PB[3]])
                        S.op("act", lambda e, q=q, ri=ri, s=s: e.copy(out=W1[:, q, ri, s, :], in_=ps[3][:, 0:128]),
                             reads=[B_su], writes=[B_su, PB[3]])
            for r in range(8):
                pr = Pk[0][:, r + 1, :].unsqueeze(2).to_broadcast([128, 16, 32])
                pi_ = Pk[1][:, r + 1, :].unsqueeze(2).to_broadcast([128, 16, 32])
                c0 = Cp[0].rearrange("p a b c -> p a (b c)")
                c1 = Cp[1].rearrange("p a b c -> p a (b c)")
                V(lambda e, pr=pr: e.tensor_tensor(out=Et[0], in0=c0, in1=pr, op=ALU.mult))
                V(lambda e, pi_=pi_: e.tensor_tensor(out=Et[1], in0=c1, in1=pi_, op=ALU.mult))
                V(lambda e, r=r: e.tensor_tensor(out=W2[:, :, r, 0, :], in0=Et[0], in1=Et[1], op=ALU.add))
                V(lambda e, pr=pr: e.tensor_tensor(out=Et[0], in0=c1, in1=pr, op=ALU.mult))
                V(lambda e, pi_=pi_: e.tensor_tensor(out=Et[1], in0=c0, in1=pi_, op=ALU.mult))
                V(lambda e, r=r: e.tensor_tensor(out=W2[:, :, r, 1, :], in0=Et[0], in1=Et[1], op=ALU.subtract))
            emit_wconv()
            S.fence()

            B_uT = Buf("uT")
            B_Hinj = [Buf("Hinj0"), Buf("Hinj1")]
            B_Wm = [Buf("Wm0"), Buf("Wm1")]
            B_Zs = [Buf("Zs0"), Buf("Zs1")]
            B_tmp = [Buf("tmp0"), Buf("tmp1")]
            B_Hb = [Buf("Hb0"), Buf("Hb1")]
            B_Hc = [Buf("Hc0"), Buf("Hc1")]
            B_tc = Buf("tc")
            B_yf = [Buf(f"yf{q}") for q in range(4)]
            B_zf2 = [[Buf(f"zf{k}_{q}") for q in range(4)] for k in range(2)]
            B_zb = [Buf(f"zb{q}") for q in range(4)]
            B_g1 = [Buf("g10"), Buf("g11")]
            B_sg = [Buf("sg0"), Buf("sg1")]
            B_sq = [Buf(f"sq{q}") for q in range(4)]
            B_rs = Buf("rs")
            B_onb = [Buf(f"onb{q}") for q in range(4)]
            uTv = uT.rearrange("p q (n r) -> p q n r", r=8)
            B_Yl = [[Buf(f"Yl{k}_{q}") for q in range(4)] for k in range(2)]

            def st_uproj(tb):
                for q in range(4):
                    pb = 2 + q % 2

                    def f_u(e, q=q, pb=pb):
                        for kc in range(8):
                            i = e.matmul(ps[pb][:, :], lhsT=wu[:, kc, q * 128:(q + 1) * 128], rhs=hT[:, kc, :],
                                         start=(kc == 0), stop=(kc == 7))
                        return i
                    S.op("pe", f_u, reads=[B_wu] + B_hT, writes=[PB[pb]])
                    S.op("act", lambda e, q=q, pb=pb: e.copy(out=uT[:, q, :], in_=ps[pb][:, :]), writes=[B_uT, PB[pb]])

            def st_lag(tb):
                kk = tb % 2
                for q in range(4):
                    pb = 4 + q % 2

                    def f_lagmm(e, q=q, pb=pb):
                        yv = ps[pb][:, :].rearrange("p (n r) -> p n r", r=8)
                        i = e.matmul(ps[pb][:, :], lhsT=LagW[:, q, 0, :], rhs=uT[:, q, :], start=True, stop=False,
                                     skip_group_check=True)
                        for tau in range(1, 8):
                            i = e.matmul(yv[:, :, tau:8], lhsT=LagW[:, q, tau, :], rhs=uTv[:, q, :, 0:8 - tau],
                                         start=False, stop=(tau == 7), skip_group_check=True)
                        return i
                    S.op("pe", f_lagmm, reads=[B_uT, B_su], writes=[PB[pb]])
                    S.op("act", lambda e, q=q, pb=pb, kk=kk: e.copy(out=Ylag[kk][:, q, :], in_=ps[pb][:, :]),
                         writes=[B_Yl[kk][q], PB[pb]])

            def st_inj(tb):
                for j in range(4):
                    def f_inj(e, j=j):
                        for q in range(4):
                            for ri in range(2):
                                c0 = (q * 2 + ri) * 64
                                for s in range(8):
                                    i = e.matmul(ps[j][:, c0:c0 + 64], lhsT=W1[32 * j:32 * j + 32, q, ri, s, :],
                                                 rhs=uTv[32 * j:32 * j + 32, q, :, s], start=(s == 0), stop=(s == 7),
                                                 tile_position=(32 * j, 0), skip_group_check=True)
                        return i
                    S.op("pe", f_inj, reads=[B_uT, B_su], writes=[PB[j]])
                    pjv = ps[j][:, :].rearrange("p (q r n) -> p q r n", q=4, r=2)
                    for ri in range(2):
                        hv = Hinj[ri].rearrange("p (q j) n -> p q j n", j=4)
                        if ri == 0:
                            S.op("act", lambda e, j=j, ri=ri, pjv=pjv, hv=hv: e.copy(out=hv[:, :, j, :], in_=pjv[:, :, ri, :]),
                                 writes=[B_Hinj[ri], PB[j]])
                        else:
                            S.op("dve", lambda e, j=j, ri=ri, pjv=pjv, hv=hv: e.tensor_copy(out=hv[:, :, j, :], in_=pjv[:, :, ri, :]),
                                 writes=[B_Hinj[ri], PB[j]])

            def st_scan_a(tb):
                lb = tb % BPS
                S.op("pool", lambda e: e.tensor_tensor(out=tmp[0], in0=Mm[1], in1=Hinj[1], op=ALU.mult),
                     reads=[B_Hinj[1], B_su], writes=[B_tmp[0]])
                S.op("pool", lambda e: e.tensor_tensor(out=Wm[0], in0=Mm[0], in1=Hinj[0], op=ALU.mult),
                     reads=[B_Hinj[0], B_su], writes=[B_Wm[0]])
                S.op("pool", lambda e: e.tensor_tensor(out=Wm[0], in0=Wm[0], in1=tmp[0], op=ALU.subtract),
                     reads=[B_tmp[0]], writes=[B_Wm[0]])
                S.op("pool", lambda e: e.tensor_tensor(out=tmp[1], in0=Mm[1], in1=Hinj[0], op=ALU.mult),
                     reads=[B_Hinj[0], B_su], writes=[B_tmp[1]])
                S.op("pool", lambda e: e.tensor_tensor(out=Wm[1], in0=Mm[0], in1=Hinj[1], op=ALU.mult),
                     reads=[B_Hinj[1], B_su], writes=[B_Wm[1]])
                S.op("pool", lambda e: e.tensor_tensor(out=Wm[1], in0=Wm[1], in1=tmp[1], op=ALU.add),
                     reads=[B_tmp[1]], writes=[B_Wm[1]])
                if lb > 0:
                    S.op("pool", lambda e: e.tensor_tensor(out=tc_[0], in0=mu[0], in1=Hc[0], op=ALU.mult),
                         reads=[B_Hc[0], B_su], writes=[B_tc])
                    S.op("pool", lambda e: e.tensor_tensor(out=tc_[1], in0=mu[1], in1=Hc[1], op=ALU.mult),
                         reads=[B_Hc[1], B_su], writes=[B_tc])
                    S.op("pool", lambda e: e.tensor_tensor(out=tc_[2], in0=mu[0], in1=Hc[1], op=ALU.mult),
                         reads=[B_Hc[1], B_su], writes=[B_tc])
                    S.op("pool", lambda e: e.tensor_tensor(out=tc_[3], in0=mu[1], in1=Hc[0], op=ALU.mult),
                         reads=[B_Hc[0], B_su], writes=[B_tc])
                    S.op("pool", lambda e: e.tensor_tensor(out=tc_[0], in0=tc_[0], in1=tc_[1], op=ALU.subtract),
                         reads=[B_tc], writes=[B_tc])
                    S.op("pool", lambda e: e.tensor_tensor(out=tc_[2], in0=tc_[2], in1=tc_[3], op=ALU.add),
                         reads=[B_tc], writes=[B_tc])
                    S.op("pool", lambda e: e.tensor_tensor(out=Wm[0][:, :, 0], in0=Wm[0][:, :, 0], in1=tc_[0], op=ALU.add),
                         reads=[B_tc, B_Wm[0]], writes=[B_Wm[0]])
                    S.op("pool", lambda e: e.tensor_tensor(out=Wm[1][:, :, 0], in0=Wm[1][:, :, 0], in1=tc_[2], op=ALU.add),
                         reads=[B_tc, B_Wm[1]], writes=[B_Wm[1]])
                for ri in range(2):
                    if lb == 0:
                        S.op("pool", lambda e, ri=ri: e.memset(Hb[ri][:, :, 0:1], 0.0), writes=[B_Hb[ri]])
                    else:
                        S.op("pool", lambda e, ri=ri: e.tensor_copy(out=Hb[ri][:, :, 0], in_=Hc[ri]),
                             reads=[B_Hc[ri]], writes=[B_Hb[ri]])

            def st_scan_b(tb):
                for ri in range(2):
                    S.op("dve", lambda e, ri=ri: e.tensor_tensor_scan(
                        out=Zs[ri].rearrange("p a b -> p (a b)"), data0=scanmask.rearrange("p a b -> p (a b)"),
                        data1=Wm[ri].rearrange("p a b -> p (a b)"), initial=0.0, op0=ALU.mult, op1=ALU.add),
                        reads=[B_Wm[ri], B_su], writes=[B_Zs[ri]])

            def st_scan_c(tb):
                S.op("pool", lambda e: e.tensor_tensor(out=tmp[0], in0=Mp[1], in1=Zs[1], op=ALU.mult),
                     reads=[B_Zs[1], B_su], writes=[B_tmp[0]])
                S.op("pool", lambda e: e.tensor_tensor(out=Wm[0], in0=Mp[0], in1=Zs[0], op=ALU.mult),
                     reads=[B_Zs[0], B_su], writes=[B_Wm[0]])
                S.op("pool", lambda e: e.tensor_tensor(out=Wm[0], in0=Wm[0], in1=tmp[0], op=ALU.subtract),
                     reads=[B_tmp[0]], writes=[B_Wm[0]])
                S.op("pool", lambda e: e.tensor_tensor(out=tmp[1], in0=Mp[1], in1=Zs[0], op=ALU.mult),
                     reads=[B_Zs[0], B_su], writes=[B_tmp[1]])
                S.op("pool", lambda e: e.tensor_tensor(out=Wm[1], in0=Mp[0], in1=Zs[1], op=ALU.mult),
                     reads=[B_Zs[1], B_su], writes=[B_Wm[1]])
                S.op("pool", lambda e: e.tensor_tensor(out=Wm[1], in0=Wm[1], in1=tmp[1], op=ALU.add),
                     reads=[B_tmp[1]], writes=[B_Wm[1]])
                for ri in range(2):
                    S.op("pool", lambda e, ri=ri: e.tensor_copy(out=Hb[ri][:, :, 1:65], in_=Wm[ri]), reads=[B_Wm[ri]], writes=[B_Hb[ri]])
                    S.op("pool", lambda e, ri=ri: e.tensor_copy(out=Hc[ri], in_=Wm[ri][:, :, 63]), reads=[B_Wm[ri]], writes=[B_Hc[ri]])

            def st_yinter_c1(tb):
                kk = tb % 2
                zf = zf2[kk]
                B_zf = B_zf2[kk]
                for q in range(4):
                    pb = 6 + q % 2

                    def f_yi(e, q=q, pb=pb):
                        yv = ps[pb][:, :].rearrange("p (n r) -> p n r", r=8)
                        for j in range(4):
                            pi_ = 4 * q + j
                            for r in range(8):
                                for ri in range(2):
                                    last = (j == 3 and r == 7 and ri == 1)
                                    i = e.matmul(yv[32 * j:32 * j + 32, :, r], lhsT=W2[:, pi_, r, ri, :], rhs=Hb[ri][:, pi_, 0:64],
                                                 start=(r == 0 and ri == 0), stop=last, tile_position=(0, 32 * j),
                                                 skip_group_check=True)
                        return i
                    S.op("pe", f_yi, reads=[B_Hb[0], B_Hb[1], B_su], writes=[PB[pb]])
                    k = q % 2
                    S.op("dve", lambda e, q=q, pb=pb, kk=kk: e.tensor_tensor(out=yf[:, q, :], in0=ps[pb][:, :], in1=Ylag[kk][:, q, :],
                                                                            op=ALU.add),
                         reads=[B_Yl[kk][q]], writes=[B_yf[q], PB[pb]])
                for q in range(4):
                    k = q % 2
                    S.op("dve", lambda e, q=q, k=k: e.tensor_tensor(out=g1[k], in0=yf[:, q, :], in1=yf[:, q, :], op=ALU.mult),
                         reads=[B_yf[q]], writes=[B_g1[k]])
                    S.op("dve", lambda e, k=k: e.tensor_scalar(out=g1[k], in0=g1[k], scalar1=0.044715, scalar2=1.0,
                                                               op0=ALU.mult, op1=ALU.add), reads=[B_g1[k]], writes=[B_g1[k]])
                    S.op("dve", lambda e, q=q, k=k: e.tensor_tensor(out=g1[k], in0=g1[k], in1=yf[:, q, :], op=ALU.mult),
                         reads=[B_g1[k], B_yf[q]], writes=[B_g1[k]])
                    S.op("act", lambda e, k=k: e.activation(out=sg[k], in_=g1[k], func=AF.Sigmoid, scale=1.5957691216057308),
                         reads=[B_g1[k]], writes=[B_sg[k]])
                    S.op("dve", lambda e, q=q, k=k: e.tensor_tensor(out=zf[:, q, :], in0=yf[:, q, :], in1=sg[k], op=ALU.mult),
                         reads=[B_yf[q], B_sg[k]], writes=[B_zf[q]])
                    S.op("act", lambda e, q=q: e.copy(out=zb[:, q, :], in_=zf[:, q, :]), reads=[B_zf[q]], writes=[B_zb[q]])

            def st_c2a(tb):
                zf = zf2[tb % 2]
                B_zf = B_zf2[tb % 2]
                for q in range(4):
                    k = q % 2
                    pb = 4 + k
                    S.op("pe", lambda e, q=q, pb=pb: e.matmul(ps[pb][:, :], lhsT=WgluB[:, q, :], rhs=zb[:, q, :], start=True, stop=True),
                         reads=[B_zb[q], B_su], writes=[PB[pb]])
                    S.op("act", lambda e, q=q, k=k, pb=pb: e.activation(out=sg[k], in_=ps[pb][:, :], func=AF.Sigmoid,
                                                                        bias=pvec[:, PV_BGLU + q:PV_BGLU + q + 1], scale=1.0),
                         reads=[B_const], writes=[B_sg[k], PB[pb]])
                    S.op("dve", lambda e, q=q, k=k: e.tensor_tensor(out=zf[:, q, :], in0=zf[:, q, :], in1=sg[k], op=ALU.mult),
                         reads=[B_sg[k], B_zf[q]], writes=[B_zf[q]])
                    S.op("act", lambda e, q=q: e.activation(out=sq[:, q, :], in_=zf[:, q, :], func=AF.Square),
                         reads=[B_zf[q]], writes=[B_sq[q]])

            def st_c2b(tb):
                zf = zf2[tb % 2]
                B_zf = B_zf2[tb % 2]

                def f_ss(e):
                    for q in range(4):
                        i = e.matmul(ps[4][:, :], lhsT=ones_f, rhs=sq[:, q, :], start=(q == 0), stop=(q == 3))
                    return i
                S.op("pe", f_ss, reads=B_sq + [B_const], writes=[PB[4]])
                S.op("act", lambda e: e.activation(out=rs, in_=ps[4][:, :], func=AF.Ln, scale=1.0 / 512, bias=pvec[:, 127:128]),
                     reads=[B_const], writes=[B_rs, PB[4]])
                S.op("act", lambda e: e.activation(out=rs, in_=rs, func=AF.Exp, scale=-0.5), reads=[B_rs], writes=[B_rs])
                for q in range(4):
                    S.op("dve", lambda e, q=q: e.scalar_tensor_tensor(
                        out=onb[:, q, :], in0=zf[:, q, :], scalar=pvec[:, PV_GCAT + 4 + q:PV_GCAT + 5 + q], in1=rs,
                        op0=ALU.mult, op1=ALU.mult), reads=[B_zf[q], B_rs, B_const], writes=[B_onb[q]])
                    dma("sp", ssmT_d[q, :, tb * TB:(tb + 1) * TB], onb[:, q, :], f"so{q}", reads=[B_onb[q]], writes=[B_ssmT])
                    if "A" not in phases:
                        dma("act", dbg_d["zf"][q, :, tb * TB:(tb + 1) * TB], zf[:, q, :], f"dz{q}", reads=[B_zf[q]])

            FR.full(0)
            st_uproj(0)
            if NBLK > 1:
                FR.full(1)
            st_lag(0)
            st_inj(0)
            if NBLK > 2:
                FR.ld(2, 0)
                FR.ld(2, 1)
            for tb in range(NBLK):
                nf = tb + 2 < NBLK
                if nf:
                    FR.st(tb + 2, 0)
                    FR.xsop(tb + 2, 0)
                    FR.st(tb + 2, 1)
                    FR.xsop(tb + 2, 1)
                if tb + 1 < NBLK:
                    st_uproj(tb + 1)
                st_scan_a(tb)
                if nf:
                    FR.tr(tb + 2, 0)
                    FR.pre(tb + 2, 2)
                    FR.tr(tb + 2, 1)
                    FR.evac(tb + 2, 0)
                    FR.pre(tb + 2, 3)
                st_scan_b(tb)
                if tb + 1 < NBLK:
                    st_lag(tb + 1)
                st_scan_c(tb)
                if nf:
                    FR.tr(tb + 2, 2)
                    FR.tr(tb + 2, 3)
                    FR.evac(tb + 2, 1)
                if tb + 1 < NBLK:
                    st_inj(tb + 1)
                if tb >= 1:
                    st_c2a(tb - 1)
                if tb + 3 < NBLK:
                    FR.ld(tb + 3, 0)
                    FR.ld(tb + 3, 1)
                st_yinter_c1(tb)
                if tb >= 1:
                    st_c2b(tb - 1)
            st_c2a(NBLK - 1)
            st_c2b(NBLK - 1)
            S.fence()

        def phase_A():
            ar.off = const_end
            dstA = x1_d if "B" in phases else y_d
            win = ar.alloc((8, INC), BF16)
            wout = ar.alloc((8, D), BF16)
            B_win, B_wout = Buf("win"), Buf("wout")
            winb_v = winb_d.rearrange("(kc p) n -> p kc n", p=128)
            for kc in range(0, 8, 2):
                dma("sp" if (kc // 2) % 2 == 0 else "act", win[:, kc:kc + 2, :], winb_v[:, kc:kc + 2, :], "win",
                    reads=[B_wcv], writes=[B_win])
            woutb_v = woutb_d.rearrange("(kc p) n -> p kc n", p=128)
            dma("sp", wout[:, 0:4, :], woutb_v[:, 0:4, :], "wout", reads=[B_wcv], writes=[B_wout])
            dma("act", wout[:, 4:8, :], woutb_v[:, 4:8, :], "wout", reads=[B_wcv], writes=[B_wout])
            FR, hT, B_hT = make_front(x_d, Am, 0, "A", nxs=2, own_junk=True)
            KT = ar.alloc((4, SEQ), BF16)
            Vp = ar.alloc((32, 8, 65), BF16)
            Fcol = ar.alloc((32, 8), F32)
            QT = [ar.alloc((4, TB), BF16) for _ in range(2)]
            NPT = 6
            PT = [ar.alloc((256,), BF16) for _ in range(NPT)]
            attn_tm = ar.alloc((4, 512), F32)
            attn_n = ar.alloc((512,), BF16)
            mixT = ar.alloc((8, TB), BF16)
            xr = [ar.alloc((D,), F32) for _ in range(2)]
            ot1 = ar.alloc((D,), F32)
            Gm = ar.alloc((D,), F32)
            fe = ar.alloc((TB,), F32, parts=8)
            fl = ar.alloc((TB,), F32, parts=8)
            Fblk = ar.alloc((TB,), F32, parts=8)
            onesrow = ar.alloc((TB,), F32, parts=8)
            carryF = ar.alloc((8,), F32, parts=8)
            negb = ar.alloc((8,), F32, parts=8)
            Fmid = ar.alloc((2, 8), F32)
            NBT = 16
            biasT = [ar.alloc((32,), F32) for _ in range(NBT)]
            trimask = ar.alloc((128,), BF16)
            rec = [ar.alloc((8,), F32) for _ in range(2)]
            accS = [ar.alloc((256,), F32) for _ in range(2)]
            B_accS = [Buf("accS0"), Buf("accS1")]
            stA = ar.alloc((8,), F32)
            print("phase A arena use", ar.off)
            B_KT = [[Buf(f"KT{j}_{l}") for l in range(BPS)] for j in range(4)]
            B_Vp = [Buf(f"Vp{l}") for l in range(BPS)]
            B_Fcol = [Buf(f"Fcol{l}") for l in range(BPS)]
            B_QT = [[Buf(f"QT{k}_{j}") for j in range(4)] for k in range(2)]
            B_PT = [Buf(f"PT{i}") for i in range(NPT)]
            B_attn = [Buf(f"attn{t}") for t in range(4)]
            B_attn_n = Buf("attn_n")
            B_mixa, B_mixs = Buf("mixa"), Buf("mixs")
            B_xr = [Buf("Axr0"), Buf("Axr1")]
            B_ot1 = Buf("Aot")
            B_Gm = Buf("Gm")
            B_fe, B_fl, B_Fblk, B_cF = Buf("fe"), Buf("fl"), Buf("Fblk"), Buf("carryF")
            B_Fmid = Buf("Fmid")
            B_bias = [Buf(f"bias{i}") for i in range(NBT)]
            B_rec = [Buf("rec0"), Buf("rec1")]
            B_stA = [Buf(f"stA{t}") for t in range(4)]
            B_Aconst = Buf("Aconst")
            dgs = [attn_tm[:, 0, 0:128], attn_tm[:, 1, 0:128]]
            B_dgs = [B_attn[0], B_attn[1]]

            S.op("pool", lambda e: e.memset(trimask, 1.0), writes=[B_Aconst])
            S.op("pool", lambda e: e.affine_select(out=trimask, in_=trimask, compare_op=ALU.is_ge, fill=0.0, base=0,
                                                   pattern=[[1, 128]], channel_multiplier=-1),
                 reads=[B_Aconst], writes=[B_Aconst])
            S.op("pool", lambda e: e.memset(onesrow, 1.0), writes=[B_Aconst])
            S.op("pool", lambda e: e.memset(Vp[:, :, :, 64:65], 1.0), writes=[B_Aconst])
            dma("sp", negb[:, 0:1], bfg_d, "c5", writes=[B_Aconst])
            S.op("dve", lambda e: e.tensor_scalar(out=negb, in0=negb, scalar1=-1.0, scalar2=None, op0=ALU.mult),
                 reads=[B_Aconst], writes=[B_Aconst])
            if "S" not in phases:
                S.op("pool", lambda e: e.memset(mixT[:, 4:8, :], 0.0), writes=[B_mixs])

            unit = 0
            cntx = 0
            nbias = 0
            def do_proj(tb):
                b = tb // BPS
                lb = tb % BPS
                qk = tb % 2
                pcnt = 0
                def f_f(e):
                    for kc in range(8):
                        i = e.matmul(ps[7][0:8, :], lhsT=win[:, kc, 1536:1544], rhs=hT[:, kc, :], start=(kc == 0), stop=(kc == 7))
                    return i
                S.op("pe", f_f, reads=[B_win] + B_hT, writes=[PB[7]])
                S.op("act", lambda e: e.activation(out=fe, in_=ps[7][0:8, :], func=AF.Exp, scale=-1.0, bias=negb[:, 0:1]),
                     reads=[B_Aconst], writes=[B_fe, PB[7]])
                S.op("act", lambda e: e.activation(out=fl, in_=fe, func=AF.Ln, bias=1.0, scale=1.0),
                     reads=[B_fe], writes=[B_fl])
                if lb == 0:
                    S.op("dve", lambda e: e.tensor_tensor_scan(out=Fblk, data0=onesrow, data1=fl, initial=0.0,
                                                               op0=ALU.mult, op1=ALU.subtract),
                         reads=[B_fl, B_Aconst], writes=[B_Fblk])
                else:
                    S.op("dve", lambda e: e.tensor_tensor_scan(out=Fblk, data0=onesrow, data1=fl, initial=carryF[:, 0:1],
                                                               op0=ALU.mult, op1=ALU.subtract),
                         reads=[B_fl, B_Aconst, B_cF], writes=[B_Fblk])
                S.op("dve", lambda e: e.tensor_copy(out=carryF[:, 0:1], in_=Fblk[:, TB - 1:TB]), reads=[B_Fblk], writes=[B_cF])

                for j in range(4):
                    pb = 2 + pcnt % 2
                    pcnt += 1

                    def f_k(e, j=j, pb=pb):
                        for kc in range(8):
                            i = e.matmul(ps[pb][:, :], lhsT=win[:, kc, 512 + j * 128:512 + (j + 1) * 128], rhs=hT[:, kc, :],
                                         start=(kc == 0), stop=(kc == 7))
                        return i
                    S.op("pe", f_k, reads=[B_win] + B_hT, writes=[PB[pb]])
                    S.op("dve", lambda e, j=j, pb=pb, lb=lb: e.tensor_copy(out=KT[:, j, lb * TB:(lb + 1) * TB], in_=ps[pb][:, :]),
                         writes=[B_KT[j][lb], PB[pb]])
                def f_ft(e):
                    for t in range(4):
                        i = e.transpose(out=ps[4][:, t * 8:(t + 1) * 8], in_=Fblk[0:8, t * 128:(t + 1) * 128],
                                        identity=ident_f[0:8, 0:8])
                    return i
                S.op("pe", f_ft, reads=[B_Fblk, B_const], writes=[PB[4]])
                S.op("dve", lambda e, lb=lb: e.tensor_copy(out=Fcol[:, 4 * lb:4 * lb + 4, :].rearrange("p a b -> p (a b)"),
                                                          in_=ps[4][:, 0:32]), writes=[B_Fcol[lb], PB[4]])

                for j in range(4):
                    pb = 2 + pcnt % 2
                    pcnt += 1

                    def f_q(e, j=j, pb=pb):
                        for kc in range(8):
                            i = e.matmul(ps[pb][:, :], lhsT=win[:, kc, j * 128:(j + 1) * 128], rhs=hT[:, kc, :],
                                         start=(kc == 0), stop=(kc == 7))
                        return i
                    S.op("pe", f_q, reads=[B_win] + B_hT, writes=[PB[pb]])
                    S.op("dve", lambda e, j=j, pb=pb, qk=qk: e.tensor_copy(out=QT[qk][:, j, :], in_=ps[pb][:, :]),
                         writes=[B_QT[qk][j], PB[pb]])
                def f_fm(e, lb=lb):
                    for c in range(2):
                        i = e.matmul(ps[4][:, 64 + c * 8:64 + (c + 1) * 8], lhsT=ones_f[0:1, :],
                                     rhs=Fcol[0:1, 4 * lb + 2 * c + 1, :], start=True, stop=True)
                    return i
                S.op("pe", f_fm, reads=[B_Fcol[lb], B_const], writes=[PB[4]])
                S.op("dve", lambda e: e.tensor_copy(out=Fmid.rearrange("p a b -> p (a b)"), in_=ps[4][:, 64:80]),
                     writes=[B_Fmid, PB[4]])

                for t in range(4):
                    pb = 2 + pcnt % 2
                    pcnt += 1

                    def f_v(e, t=t, pb=pb):
                        for kc in range(8):
                            i = e.matmul(ps[pb][:, :], lhsT=hT[:, kc, t * 128:(t + 1) * 128], rhs=win[:, kc, 1024:1536],
                                         start=(kc == 0), stop=(kc == 7))
                        return i
                    S.op("pe", f_v, reads=[B_win] + B_hT, writes=[PB[pb]])
                    S.op("dve", lambda e, t=t, pb=pb, lb=lb: e.tensor_copy(
                        out=Vp[:, 4 * lb + t, :, 0:64], in_=ps[pb][:, :].rearrange("p (h d) -> p h d", h=8)),
                        writes=[B_Vp[lb], PB[pb]])
            def do_attn(tb):
                nonlocal unit, nbias
                b = tb // BPS
                lb = tb % BPS
                qk = tb % 2
                units = []
                for c in range(2):
                    i2 = 2 * lb + c
                    nj = 2 * i2 + 2
                    for jp in range(4):
                        for jb in range(nj):
                            for hh in range(2):
                                units.append((c, 2 * jp + hh, jb, nj))
                LOOK = 2
                ust = {}
                for c in range(2):
                    njc = 2 * (2 * lb + c) + 2
                    for h in range(8):
                        bi = c * 8 + h
                        ust[(c, h)] = bi
                        S.op("dve", lambda e, bi=bi, njc=njc, h=h, c=c: e.tensor_scalar(
                            out=biasT[bi][:, 0:njc], in0=Fcol[:, 0:njc, h], scalar1=-1.0, scalar2=Fmid[:, c, h:h + 1],
                            op0=ALU.mult, op1=ALU.add),
                            reads=[B_Fcol[l] for l in range(lb + 1)] + [B_Fmid], writes=[B_bias[bi]])

                def emit_st(ui, c, h, jb, nj):
                    nonlocal unit, nbias
                    j = h // 2
                    r0 = 64 * (h % 2)
                    bi = ust[(c, h)]
                    sb = (2, 3, 4, 7, 0, 1)[unit % 6]
                    pi = unit % NPT
                    unit += 1
                    diag1 = (jb == nj - 1)
                    diag0 = (jb == nj - 2)
                    q0 = c * 256 + (128 if diag1 else 0)
                    ncol = 128 if diag1 else 256
                    klb = jb // 4
                    S.op("pe", lambda e, sb=sb, r0=r0, j=j, jb=jb, q0=q0, ncol=ncol, qk=qk: e.matmul(
                        ps[sb][:, 0:ncol], lhsT=KT[r0:r0 + 64, j, jb * 128:(jb + 1) * 128],
                        rhs=QT[qk][r0:r0 + 64, j, q0:q0 + ncol], start=True, stop=True),
                        reads=[B_KT[j][klb], B_QT[qk][j]], writes=[PB[sb]])
                    S.op("act", lambda e, sb=sb, pi=pi, ncol=ncol, bi=bi, jb=jb: e.activation(
                        out=PT[pi][:, 0:ncol], in_=ps[sb][:, 0:ncol], func=AF.Exp, scale=0.125,
                        bias=biasT[bi][:, jb:jb + 1]),
                        reads=[B_bias[bi]], writes=[B_PT[pi], PB[sb]])
                    if diag0 or diag1:
                        S.op("dve", lambda e, pi=pi: e.tensor_tensor(out=PT[pi][:, 0:128], in0=PT[pi][:, 0:128],
                                                                     in1=trimask, op=ALU.mult),
                             reads=[B_Aconst], writes=[B_PT[pi]])
                    return pi

                def emit_pv(pi, c, h, jb, nj):
                    ab = 5 + h % 2
                    diag1 = (jb == nj - 1)
                    klb = jb // 4

                    def f_pv(e, pi=pi, jb=jb, h=h, ab=ab, diag1=diag1, nj=nj):
                        if diag1:
                            i = e.matmul(ps[ab][:, 128:193], lhsT=PT[pi][:, 0:128], rhs=Vp[:, jb, h, :],
                                         start=False, stop=True, skip_group_check=True)
                        else:
                            e.matmul(ps[ab][:, 0:65], lhsT=PT[pi][:, 0:128], rhs=Vp[:, jb, h, :],
                                     start=(jb == 0), stop=(jb == nj - 2), skip_group_check=True)
                            i = e.matmul(ps[ab][:, 128:193], lhsT=PT[pi][:, 128:256], rhs=Vp[:, jb, h, :],
                                         start=False, stop=False, skip_group_check=True)
                        return i
                    S.op("pe", f_pv, reads=[B_PT[pi], B_Vp[klb], B_Aconst], writes=[PB[ab]])
                    if jb == nj - 1:
                        rk = h % 2
                        S.op("dve", lambda e, rk=rk, ab=ab: e.tensor_copy(out=accS[rk], in_=ps[ab][:, 0:256]),
                             writes=[B_accS[rk], PB[ab]])
                        accv = accS[rk].rearrange("p (c d) -> p c d", c=2)
                        S.op("dve", lambda e, rk=rk, accv=accv: e.reciprocal(out=rec[rk][:, 0:2], in_=accv[:, :, 64]),
                             reads=[B_accS[rk]], writes=[B_rec[rk]])
                        for cc in range(2):
                            t = 2 * c + cc
                            S.op("dve", lambda e, rk=rk, accv=accv, cc=cc, t=t, h=h: e.tensor_scalar(
                                out=attn_tm[:, t, h * 64:(h + 1) * 64], in0=accv[:, cc, 0:64], scalar1=rec[rk][:, cc:cc + 1],
                                scalar2=None, op0=ALU.mult),
                                reads=[B_rec[rk], B_accS[rk]], writes=[B_attn[t]])

                pis = []
                nstep = len(units) // 2
                LK = 2
                for st in range(nstep + LK):
                    if st < nstep:
                        pis.append(emit_st(2 * st, *units[2 * st]))
                        pis.append(emit_st(2 * st + 1, *units[2 * st + 1]))
                    if st >= LK:
                        s0 = st - LK
                        emit_pv(pis[2 * s0], *units[2 * s0])
                        emit_pv(pis[2 * s0 + 1], *units[2 * s0 + 1])
            def do_out(tb):
                nonlocal cntx
                b = tb // BPS
                lb = tb % BPS
                qk = tb % 2
                pv0 = ps[0][:, :].bitcast(BF16).rearrange("p (kc t) -> p kc t", kc=4)
                for half in range(2):
                    for t2 in range(2):
                        t = half * 2 + t2
                        S.op("act", lambda e, t=t: e.activation(out=ot1[:, 0:512], in_=attn_tm[:, t, :], func=AF.Square,
                                                                 accum_out=stA[:, t:t + 1]),
                             reads=[B_attn[t]], writes=[B_ot1, B_stA[t]])
                        S.op("act", lambda e, t=t: e.activation(out=stA[:, 4 + t:5 + t], in_=stA[:, t:t + 1], func=AF.Ln,
                                                                 scale=1.0 / 512, bias=pvec[:, 127:128]),
                             reads=[B_stA[t], B_const], writes=[B_stA[t]])
                        S.op("act", lambda e, t=t: e.activation(out=stA[:, t:t + 1], in_=stA[:, 4 + t:5 + t], func=AF.Exp,
                                                                 scale=-0.5), reads=[B_stA[t]], writes=[B_stA[t]])
                        S.op("dve", lambda e, t=t: e.tensor_scalar(out=attn_n, in0=attn_tm[:, t, :], scalar1=stA[:, t:t + 1],
                                                                   scalar2=None, op0=ALU.mult),
                             reads=[B_attn[t], B_stA[t]], writes=[B_attn_n])

                        def f_tr2(e, t2=t2):
                            for kc in range(4):
                                i = e.transpose(out=pv0[:, kc, t2 * 128:(t2 + 1) * 128], in_=attn_n[:, kc * 128:(kc + 1) * 128],
                                                identity=ident_b)
                            return i
                        S.op("pe", f_tr2, reads=[B_attn_n, B_const], writes=[PB[0]])
                    for kc in range(4):
                        S.op("dve", lambda e, kc=kc, half=half: e.tensor_scalar(
                            out=mixT[:, kc, half * 256:(half + 1) * 256], in0=pv0[:, kc, :],
                            scalar1=pvec[:, PV_GCAT + kc:PV_GCAT + kc + 1], scalar2=None, op0=ALU.mult),
                            reads=[B_const], writes=[B_mixa, PB[0]])
                if "S" in phases:
                    for q in range(4):
                        dma("act", mixT[:, 4 + q, :], ssmT_d[q, :, tb * TB:(tb + 1) * TB], "mixs", writes=[B_mixs])
                for t in range(4):
                    k = cntx % 2
                    cntx += 1
                    rr = tb * TB + t * 128
                    dma("act", xr[k], x_d[rr:rr + 128, :], f"Axr{k}", writes=[B_xr[k]])
                    for nb in range(2):
                        pb = 2 + nb

                        def f_o(e, t=t, nb=nb, pb=pb):
                            for kc in range(8):
                                i = e.matmul(ps[pb][:, :], lhsT=mixT[:, kc, t * 128:(t + 1) * 128],
                                             rhs=wout[:, kc, nb * 512:(nb + 1) * 512], start=(kc == 0), stop=(kc == 7))
                            return i
                        S.op("pe", f_o, reads=[B_mixa, B_mixs, B_wout], writes=[PB[pb]])
                        S.op("dve", lambda e, nb=nb, pb=pb: e.tensor_tensor(
                            out=ot1[:, nb * 512:(nb + 1) * 512], in0=ps[pb][:, :], in1=Gm[:, nb * 512:(nb + 1) * 512],
                            op=ALU.mult), reads=[B_Gm], writes=[B_ot1, PB[pb]])
                    S.op("dve", lambda e, k=k: e.tensor_tensor(out=xr[k], in0=ot1, in1=xr[k], op=ALU.add),
                         reads=[B_ot1], writes=[B_xr[k]])
                    dma("sp", dstA[rr:rr + 128, :], xr[k], f"Axo{k}", reads=[B_xr[k]], writes=[B_x1s])
            FR.full(0)
            do_proj(0)
            FR.full(1)
            for tb in range(NBLK):
                if tb % BPS == 0:
                    build_G(Gm, B_Gm, 16, tb // BPS, dgs, B_dgs)
                do_attn(tb)
                nf = tb + 2 < NBLK
                if nf:
                    FR.pre(tb + 2, 0)
                    FR.pre(tb + 2, 1)
                if tb + 1 < NBLK:
                    do_proj(tb + 1)
                if nf:
                    FR.tr(tb + 2, 0)
                    FR.pre(tb + 2, 2)
                    FR.tr(tb + 2, 1)
                    FR.evac(tb + 2, 0)
                    FR.pre(tb + 2, 3)
                do_out(tb)
                if nf:
                    FR.tr(tb + 2, 2)
                    FR.tr(tb + 2, 3)
                    FR.evac(tb + 2, 1)
            S.fence()

        def phase_B():
            ar.off = const_end
            srcB = x1_d if "A" in phases else x_d
            wup = ar.alloc((8, 2 * DFF), BF16)
            wdn = ar.alloc((NFC, D), BF16)
            B_wup, B_wdn = Buf("wup"), Buf("wdn")
            wupb_v = wupb_d.rearrange("(kc p) n -> p kc n", p=128)
            NG = (NFC + 3) // 4
            B_wupg = [Buf(f"wup{g}") for g in range(NG)]
            for g in range(NG):
                c0, c1 = g * 512, min(DFF, (g + 1) * 512)
                dma("sp", wup[:, :, c0:c1], wupb_v[:, :, c0:c1], f"wupg{g}", reads=[B_wcv], writes=[B_wupg[g]])
                dma("act", wup[:, :, DFF + c0:DFF + c1], wupb_v[:, :, DFF + c0:DFF + c1], f"wupv{g}", reads=[B_wcv],
                    writes=[B_wupg[g]])
            wdnb_v = wdnb_d.rearrange("(fc p) n -> p fc n", p=128)
            for i_, g0 in enumerate(range(0, NFC, 6)):
                g1_ = min(NFC, g0 + 6)
                dma("sp" if i_ % 2 == 0 else "act", wdn[:, g0:g1_, :], wdnb_v[:, g0:g1_, :], "wdn", reads=[B_wcv], writes=[B_wdn])
            FR, hT, B_hT = make_front(srcB, Af, 24, "B")
            actT = ar.alloc((NFC, TB), BF16)
            B_actT = Buf("actT")
            gsb = [ar.alloc((TB + 2,), F32) for _ in range(2)]
            ct1 = [ar.alloc((TB,), F32) for _ in range(2)]
            ct2 = [ar.alloc((TB,), F32) for _ in range(2)]
            B_gsb = [Buf("gsb0"), Buf("gsb1")]
            B_ct1 = [Buf("ct10"), Buf("ct11")]
            B_ct2 = [Buf("ct20"), Buf("ct21")]
            carry = ar.alloc((NFC, 2), F32)
            B_carry = [Buf("carry%d" % i) for i in range(NFC)]
            xr = [ar.alloc((D,), F32) for _ in range(2)]
            ot1 = ar.alloc((D,), F32)
            st2 = ar.alloc((8,), F32)
            Gf = ar.alloc((D,), F32)
            dgs = [gsb[0][:, 0:128], gsb[1][:, 0:128]]
            B_dgs = B_gsb
            B_Gf = Buf("Gf")
            B_xr = [Buf("xr0"), Buf("xr1")]
            B_ot1 = Buf("ot")
            B_st2 = [Buf("st20"), Buf("st21")]
            print("phase B arena use", ar.off)
            cnt2 = 0
            pend_tail = [None]
            for tb in range(NBLK):
                b = tb // BPS
                lb = tb % BPS
                if lb == 0:
                    build_G(Gf, B_Gf, 40, b, dgs, B_dgs)
                if tb == 0:
                    FR.full(tb)
                for fc in range(NFC):
                    if tb + 1 < NBLK and fc in (8, 12):
                        FR.ld(tb + 1, (fc - 8) // 4)
                        FR.st(tb + 1, (fc - 8) // 4)
                    if tb + 1 < NBLK and fc == 17:
                        FR.xsop(tb + 1, 0)
                    k = fc % 2
                    pg, pvb = (2, 3, 6)[fc % 3], (4, 5, 7)[fc % 3]

                    def f_up(e, fc=fc, pg=pg, pvb=pvb):
                        for kc in range(8):
                            e.matmul(ps[pg][:, :], lhsT=wup[:, kc, fc * 128:(fc + 1) * 128], rhs=hT[:, kc, :],
                                     start=(kc == 0), stop=(kc == 7))
                        for kc in range(8):
                            i = e.matmul(ps[pvb][:, :], lhsT=wup[:, kc, DFF + fc * 128:DFF + (fc + 1) * 128],
                                         rhs=hT[:, kc, :], start=(kc == 0), stop=(kc == 7))
                        return i
                    S.op("pe", f_up, reads=[B_wupg[fc // 4]] + B_hT, writes=[PB[pg], PB[pvb]])
                    if lb == 0:
                        S.op("act", lambda e, k=k: e.activation(out=gsb[k][:, 0:2], in_=gsb[k][:, 0:2], func=AF.Copy, scale=0.0),
                             writes=[B_gsb[k]])
                    else:
                        S.op("act", lambda e, k=k, fc=fc: e.copy(out=gsb[k][:, 0:2], in_=carry[:, fc, :]),
                             reads=[B_carry[fc]], writes=[B_gsb[k]])
                    S.op("act", lambda e, k=k, pg=pg: e.copy(out=gsb[k][:, 2:TB + 2], in_=ps[pg][:, :]),
                         writes=[B_gsb[k], PB[pg]])
                    S.op("act", lambda e, fc=fc, pg=pg: e.copy(out=carry[:, fc, :], in_=ps[pg][:, TB - 2:TB]),
                         writes=[B_carry[fc], PB[pg]])
                    cw = PV_CW + fc * 3
                    S.op("act", lambda e, k=k, fc=fc, cw=cw, pg=pg: e.activation(
                        out=ct1[k], in_=ps[pg][:, :], func=AF.Identity, scale=pvec[:, cw + 2:cw + 3],
                        bias=pvec[:, PV_CB + fc:PV_CB + fc + 1]),
                        reads=[B_const], writes=[B_ct1[k], PB[pg]])
                    S.op("dve", lambda e, k=k, cw=cw: e.scalar_tensor_tensor(
                        out=ct2[k], in0=gsb[k][:, 1:TB + 1], scalar=pvec[:, cw + 1:cw + 2], in1=ct1[k],
                        op0=ALU.mult, op1=ALU.add), reads=[B_gsb[k], B_ct1[k], B_const], writes=[B_ct2[k]])
                    S.op("dve", lambda e, k=k, cw=cw: e.scalar_tensor_tensor(
                        out=ct1[k], in0=gsb[k][:, 0:TB], scalar=pvec[:, cw:cw + 1], in1=ct2[k],
                        op0=ALU.mult, op1=ALU.add), reads=[B_gsb[k], B_ct2[k], B_const], writes=[B_ct1[k]])
                    def tail(fc=fc, k=k, pvb=pvb):
                        S.op("act", lambda e, k=k: e.activation(out=ct2[k], in_=ct1[k], func=AF.Silu),
                             reads=[B_ct1[k]], writes=[B_ct2[k]])
                        S.op("dve", lambda e, k=k, fc=fc, pvb=pvb: e.tensor_tensor(out=actT[:, fc, :], in0=ct2[k],
                                                                                  in1=ps[pvb][:, :], op=ALU.mult),
                             reads=[B_ct2[k]], writes=[B_actT, PB[pvb]])
                    if pend_tail[0] is not None:
                        pend_tail[0]()
                    pend_tail[0] = tail
                pend_tail[0]()
                pend_tail[0] = None
                for t in range(4):
                    if tb + 1 < NBLK:
                        FR.tr(tb + 1, t)
                        if t % 2 == 1:
                            FR.evac(tb + 1, t // 2)
                        if t < 2:
                            FR.ld(tb + 1, t + 2)
                            FR.st(tb + 1, t + 2)
                        if t < 3:
                            FR.xsop(tb + 1, t + 1)
                    k = cnt2 % 2
                    cnt2 += 1
                    r0 = tb * TB + t * 128
                    dma("act", xr[k], srcB[r0:r0 + 128, :], f"xr{k}", writes=[B_xr[k]])
                    for nb in range(2):
                        pb = 6 + nb

                        def f_dn(e, t=t, nb=nb, pb=pb):
                            for fc in range(NFC):
                                i = e.matmul(ps[pb][:, :], lhsT=actT[:, fc, t * 128:(t + 1) * 128],
                                             rhs=wdn[:, fc, nb * 512:(nb + 1) * 512], start=(fc == 0), stop=(fc == NFC - 1))
                            return i
                        S.op("pe", f_dn, reads=[B_actT, B_wdn], writes=[PB[pb]])
                        S.op("dve", lambda e, nb=nb, pb=pb: e.tensor_tensor(
                            out=ot1[:, nb * 512:(nb + 1) * 512], in0=ps[pb][:, :], in1=Gf[:, nb * 512:(nb + 1) * 512],
                            op=ALU.mult), reads=[B_Gf], writes=[B_ot1, PB[pb]])
                        S.op("dve", lambda e, nb=nb, k=k: e.tensor_tensor(
                            out=xr[k][:, nb * 512:(nb + 1) * 512], in0=ot1[:, nb * 512:(nb + 1) * 512],
                            in1=xr[k][:, nb * 512:(nb + 1) * 512], op=ALU.add), reads=[B_ot1], writes=[B_xr[k]])
                    junkb = ct1[0].bitcast(BF16)
                    S.op("act", lambda e, k=k: e.activation(out=junkb, in_=xr[k], func=AF.Square,
                                                            accum_out=st2[:, k:k + 1]),
                         reads=[B_xr[k]], writes=[B_ct1[0], B_st2[k]])
                    S.op("act", lambda e, k=k: e.activation(out=st2[:, 4 + k:5 + k], in_=st2[:, k:k + 1], func=AF.Ln,
                                                            scale=1.0 / D, bias=pvec[:, 127:128]),
                         reads=[B_st2[k], B_const], writes=[B_st2[k]])
                    S.op("act", lambda e, k=k: e.activation(out=st2[:, k:k + 1], in_=st2[:, 4 + k:5 + k], func=AF.Exp,
                                                            scale=-0.5), reads=[B_st2[k]], writes=[B_st2[k]])
                    S.op("dve", lambda e, k=k: e.scalar_tensor_tensor(out=xr[k], in0=xr[k], scalar=st2[:, k:k + 1],
                                                                      in1=Gfin, op0=ALU.mult, op1=ALU.mult),
                         reads=[B_st2[k], B_const], writes=[B_xr[k]])
                    dma("sp", y_d[r0:r0 + 128, :], xr[k], f"xo{k}", reads=[B_xr[k]])
            S.fence()

        if "S" in phases:
            phase_S()
        if "A" in phases:
            phase_A()
        if "B" in phases:
            phase_B()
        S.emit()
    return nc


def _fm(v, n):
    return np.ascontiguousarray(np.asarray(v, np.float32).reshape(n, 128).T)


def prep_shared(inp):
    f = lambda a: np.ascontiguousarray(np.asarray(a, np.float32))
    sh = {}
    sh["w_ada"] = f(inp["w_ada"][0])
    sh["w_in"] = f(inp["w_in"][0])
    sh["w_out"] = f(inp["w_out"][0])
    sh["w_up"] = f(inp["w_up"][0])
    sh["w_down"] = f(inp["w_down"][0])
    sh["b_ada2"] = np.ascontiguousarray(np.broadcast_to(f(inp["b_ada"][0])[None, :], (NB, 6 * D)))
    pv = np.zeros((128, 128), np.float32)
    pv[:, 0:8] = _fm(inp["g_mix"][0], 8)
    pv[:, 8:16] = _fm(inp["g_ffn"][0], 8)
    pv[:, 16:20] = _fm(inp["g_attn_out"][0], 4)
    pv[:, 20:24] = _fm(inp["g_ssm_out"][0], 4)
    cw = f(inp["conv_w"][0])
    pv[:, 24:90] = np.ascontiguousarray(cw.reshape(3, NFC, 128).transpose(2, 1, 0)).reshape(128, NFC * 3)
    pv[:, 90:112] = _fm(inp["conv_b"][0], NFC)
    pv[:, 112:116] = _fm(f(inp["d_skip"][0]).reshape(-1), 4)
    pv[:, 116:120] = _fm(f(inp["b_glu"][0]).reshape(-1), 4)
    pv[:, 127] = EPS
    sh["pvec"] = pv
    sh["g_final"] = f(inp["g_final"]).reshape(1, D)
    sh["b_fgate"] = f(inp["b_fgate"][0]).reshape(8, 1)
    sel = np.zeros((NB, NB * 128), np.float32)
    for b in range(NB):
        sel[b, b * 128:(b + 1) * 128] = 1.0
    sh["sel"] = sel
    def pl(a):
        a = f(a)
        rest = a.shape[2:]
        a = a.reshape((16, 2, 64) + rest)
        a = np.moveaxis(a, 0, 2)
        return np.ascontiguousarray(a.reshape((128, 16) + rest))
    are = pl(inp["a_re"][0])
    aim = pl(inp["a_im"][0])
    ldt = pl(np.broadcast_to(f(inp["log_dt"][0])[:, None], (32, 64)))
    sh["ssm16"] = np.ascontiguousarray(np.concatenate([are, aim, ldt], axis=1))
    bre = pl(inp["ssm_b_re"][0]).reshape(128, 256)
    bim = pl(inp["ssm_b_im"][0]).reshape(128, 256)
    sh["ssmB"] = np.ascontiguousarray(np.concatenate([bre, bim], axis=1))
    cre = pl(np.transpose(f(inp["ssm_c_re"][0]), (0, 2, 1))).reshape(128, 256)
    cim = pl(np.transpose(f(inp["ssm_c_im"][0]), (0, 2, 1))).reshape(128, 256)
    sh["ssmC"] = np.ascontiguousarray(np.concatenate([cre, cim], axis=1))
    wg = f(inp["w_glu"][0])
    blk = np.zeros((128, 4, 128), np.float32)
    for g in range(32):
        q, g8 = divmod(g, 8)
        blk[g8 * 16:(g8 + 1) * 16, q, g8 * 16:(g8 + 1) * 16] = wg[g]
    sh["wglu_blk"] = blk.reshape(128, 512)
    return sh


_NC_CACHE = {}


def kernel(**inp):
    if "full" not in _NC_CACHE:
        _NC_CACHE["full"] = build("SAB")
    nc = _NC_CACHE["full"]
    sh = prep_shared(inp)
    x = np.asarray(inp["x"], np.float32)
    c = np.asarray(inp["c"], np.float32)
    in_maps = []
    for i in range(8):
        m = dict(sh)
        m["x"] = np.ascontiguousarray(x[NB * i:NB * (i + 1)].reshape(NTOK, D))
        m["c"] = np.ascontiguousarray(c[NB * i:NB * (i + 1)])
        in_maps.append(m)
    res = run_bass_kernel_spmd(nc, in_maps, core_ids=list(range(8)))
    out = np.concatenate([res.results[i]["y"].reshape(NB, SEQ, D) for i in range(8)], axis=0)
    return out.astype(np.float32)
```

```python
import numpy as np
from contextlib import ExitStack
import concourse.bass as bass
import concourse.mybir as mybir
from concourse.bass_utils import run_bass_kernel_spmd

F32 = mybir.dt.float32
BF16 = mybir.dt.bfloat16
AF = mybir.ActivationFunctionType
ALU = mybir.AluOpType

D = 1024
SEQ = 4096
NB = 2
NTOK = NB * SEQ
TB = 512
NBLK = NTOK // TB
BPS = SEQ // TB
DFF = 2816
NFC = DFF // 128
INC = 2056
EPS = 1e-6
AW = 53000


class Buf:
    __slots__ = ("name", "wset", "readers")

    def __init__(self, name):
        self.name = name
        self.wset = []
        self.readers = []


class Op:
    __slots__ = ("id", "eng", "fn", "deps", "raw", "needs_inc", "tick", "dma_key", "dma_val")


class Sched:
    ENGS = ("pe", "act", "dve", "pool", "sp")

    def __init__(self, nc):
        self.nc = nc
        self.ops = []
        self.per_eng = {e: [] for e in self.ENGS}
        self.dma_cnt = {}

    def op(self, eng, fn, reads=(), writes=(), dma_key=None):
        o = Op()
        o.id = len(self.ops)
        o.eng = eng
        o.fn = fn
        o.needs_inc = False
        o.tick = None
        o.dma_key = dma_key
        o.dma_val = None
        deps = set()
        raw = set()
        for b in reads:
            deps.update(b.wset)
            raw.update(b.wset)
        for b in writes:
            deps.update(b.wset)
            deps.update(b.readers)
        for b in reads:
            b.readers.append(o.id)
        for b in writes:
            if b.readers:
                b.wset = [o.id]
                b.readers = []
            else:
                b.wset.append(o.id)
                if len(b.wset) > 6:
                    b.wset = b.wset[-6:]
        deps.discard(o.id)
        raw.discard(o.id)
        o.deps = deps
        o.raw = raw
        if dma_key is not None:
            self.dma_cnt[dma_key] = self.dma_cnt.get(dma_key, 0) + 1
            o.dma_val = 16 * self.dma_cnt[dma_key]
        self.ops.append(o)
        self.per_eng[eng].append(o)
        return o

    def fence(self):
        last = []
        for e in self.ENGS:
            for o in reversed(self.per_eng[e]):
                if o.dma_key is None and o.fn is not None:
                    last.append(o.id)
                    break
        lastd = {}
        for o in self.ops:
            if o.dma_key is not None:
                lastd[o.dma_key] = o.id
        deps = set(last) | set(lastd.values())
        for e in self.ENGS:
            o = Op()
            o.id = len(self.ops)
            o.eng = e
            o.fn = None
            o.deps = set(deps)
            o.raw = set(deps)
            o.needs_inc = False
            o.tick = None
            o.dma_key = None
            o.dma_val = None
            self.ops.append(o)
            self.per_eng[e].append(o)

    def emit(self):
        nc = self.nc
        ops = self.ops
        for o in ops:
            for d in o.deps:
                dd = ops[d]
                if dd.dma_key is None:
                    if dd.eng == o.eng and o.dma_key is None and d not in o.raw:
                        continue
                    dd.needs_inc = True
        ticks = {e: 0 for e in self.ENGS}
        for e in self.ENGS:
            for o in self.per_eng[e]:
                if o.dma_key is None and o.needs_inc:
                    ticks[e] += 1
                    o.tick = ticks[e]
        with ExitStack() as es:
            esem = {e: es.enter_context(nc.semaphore("s_" + e)) for e in self.ENGS}
            dsem = {k: es.enter_context(nc.semaphore("d_" + str(k))) for k in self.dma_cnt}
            block = es.enter_context(nc.Block())

            def run(ename, eng):
                seen = {}
                for o in self.per_eng[ename]:
                    need = {}
                    for d in o.deps:
                        dd = ops[d]
                        if dd.dma_key is not None:
                            key = ("d", dd.dma_key)
                            val = dd.dma_val
                        else:
                            if dd.eng == ename and o.dma_key is None and d not in o.raw:
                                continue
                            key = ("e", dd.eng)
                            val = dd.tick
                        if seen.get(key, 0) >= val:
                            continue
                        if need.get(key, 0) < val:
                            need[key] = val
                    for key, val in need.items():
                        sem = dsem[key[1]] if key[0] == "d" else esem[key[1]]
                        eng.wait_ge(sem, val)
                        seen[key] = val
                    if o.fn is None:
                        continue
                    inst = o.fn(eng)
                    if o.dma_key is not None:
                        inst.then_inc(dsem[o.dma_key], 16)
                    elif o.needs_inc:
                        inst.then_inc(esem[ename], 1)
                if ename == "sp":
                    for k, c in self.dma_cnt.items():
                        if seen.get(("d", k), 0) < 16 * c:
                            eng.wait_ge(dsem[k], 16 * c)

            @block.tensor
            def _(e):
                run("pe", e)

            @block.scalar
            def _(e):
                run("act", e)

            @block.vector
            def _(e):
                run("dve", e)

            @block.gpsimd
            def _(e):
                run("pool", e)

            @block.sync
            def _(e):
                run("sp", e)


class Arena:
    def __init__(self, ap_f32):
        self.ap = ap_f32
        self.W = ap_f32.shape[1]
        self.off = 0

    def alloc(self, shape, dtype, parts=128):
        if isinstance(shape, int):
            shape = (shape,)
        n = int(np.prod(shape))
        esz = 4 if dtype == F32 else 2
        words = (n * esz + 3) // 4
        words = (words + 7) // 8 * 8
        assert self.off + words <= self.W, f"arena overflow {self.off}+{words}>{self.W}"
        a = self.ap[0:parts, self.off:self.off + words]
        self.off += words
        if dtype != F32:
            a = a.bitcast(dtype)
        a = a[:, 0:n]
        if len(shape) > 1:
            names = " ".join(f"d{i}" for i in range(len(shape)))
            kw = {f"d{i}": int(s) for i, s in enumerate(shape)}
            a = a.rearrange(f"p ({names}) -> p {names}", **kw)
        return a


def build(phases="SAB", dbg=()):
    nc = bass.Bass("TRN2", target_bir_lowering=False)

    def din(name, shape, dt=F32):
        return nc.dram_tensor(name, list(shape), dt, kind="ExternalInput").ap()

    x_d = din("x", [NTOK, D])
    c_d = din("c", [NB, D])
    wada_d = din("w_ada", [D, 6 * D])
    bada_d = din("b_ada2", [NB, 6 * D])
    win_d = din("w_in", [D, INC])
    wout_d = din("w_out", [D, D])
    wup_d = din("w_up", [D, 2 * DFF])
    wdn_d = din("w_down", [DFF, D])
    pvec_d = din("pvec", [128, 128])
    gfin_d = din("g_final", [1, D])
    bfg_d = din("b_fgate", [8, 1])
    sel_d = din("sel", [NB, NB * 128])
    ssm16_d = din("ssm16", [128, 3 * 16])
    ssmB_d = din("ssmB", [128, 2 * 256])
    ssmC_d = din("ssmC", [128, 2 * 256])
    wglu_d = din("wglu_blk", [128, 4 * 128])
    y_d = nc.dram_tensor("y", [NTOK, D], F32, kind="ExternalOutput").ap()
    x1_d = nc.dram_tensor("x1s", [NTOK, D], F32).ap()
    ssmT_d = nc.dram_tensor("ssmT", [4, 128, NTOK], BF16).ap()
    winb_d = nc.dram_tensor("win_b", [D, INC], BF16).ap()
    woutb_d = nc.dram_tensor("wout_b", [D, D], BF16).ap()
    wupb_d = nc.dram_tensor("wup_b", [D, 2 * DFF], BF16).ap()
    wdnb_d = nc.dram_tensor("wdn_b", [DFF, D], BF16).ap()
    dbg_d = {}
    for name, shape in dbg:
        dbg_d[name] = nc.dram_tensor(name, list(shape), F32, kind="ExternalOutput").ap()

    with ExitStack() as es:
        arena_t = es.enter_context(nc.sbuf_tensor("arena", [128, AW], F32))
        ps = [es.enter_context(nc.psum_tensor(f"ps{i}", [128, 512], F32)) for i in range(8)]
        PB = [Buf(f"ps{i}") for i in range(8)]
        S = Sched(nc)
        ar = Arena(arena_t[:])

        def dma(q, out, in_, key, reads=(), writes=()):
            S.op(q, lambda e: e.dma_start(out=out, in_=in_), reads=reads, writes=writes, dma_key=key)

        ident_f = ar.alloc((128,), F32)
        ident_b = ar.alloc((128,), BF16)
        ones_f = ar.alloc((128,), F32)
        pvec = ar.alloc((128,), F32)
        modT = ar.alloc((48, NB), F32)
        Am = ar.alloc((8, NB), F32)
        Af = ar.alloc((8, NB), F32)
        Gfin = ar.alloc((D,), F32)
        B_const = Buf("const")
        B_x1s = Buf("x1s")
        B_ssmT = Buf("ssmT")
        B_mod = Buf("mod")
        const_end = ar.off

        PV_GMIX, PV_GFFN, PV_GCAT, PV_CW, PV_CB, PV_DSK, PV_BGLU = 0, 8, 16, 24, 90, 112, 116

        S.op("pool", lambda e: e.memset(ident_f, 0.0), writes=[B_const])
        S.op("pool", lambda e: e.affine_select(out=ident_f, in_=ident_f, compare_op=ALU.not_equal, fill=1.0,
                                               base=0, pattern=[[-1, 128]], channel_multiplier=1),
             reads=[B_const], writes=[B_const])
        S.op("pool", lambda e: e.tensor_copy(out=ident_b, in_=ident_f), reads=[B_const], writes=[B_const])
        S.op("pool", lambda e: e.memset(ones_f, 1.0), writes=[B_const])
        dma("sp", pvec, pvec_d, "c0", writes=[B_const])
        dma("sp", Gfin, gfin_d.partition_broadcast(128), "c2", writes=[B_const])

        B_wcv = Buf("wconv")

        def emit_wconv():
            for kc in range(8):
                r = slice(kc * 128, (kc + 1) * 128)
                dma("pool", winb_d[r, :], win_d[r, :], "wcv", writes=[B_wcv])
            for kc in range(0, 8, 2):
                r = slice(kc * 128, (kc + 2) * 128)
                dma("pool", woutb_d[r, :], wout_d[r, :], "wcv", writes=[B_wcv])
            for kc in range(8):
                r = slice(kc * 128, (kc + 1) * 128)
                dma("pool", wupb_d[r, :], wup_d[r, :], "wcv", writes=[B_wcv])
            for fc in range(0, NFC, 2):
                r = slice(fc * 128, (fc + 2) * 128)
                dma("pool", wdnb_d[r, :], wdn_d[r, :], "wcv", writes=[B_wcv])

        setup_off = ar.off
        c_sb = ar.alloc((D,), F32, parts=NB)
        sc_sb = ar.alloc((D,), F32, parts=NB)
        scT = ar.alloc((8, NB), F32)
        mod_tm = ar.alloc((6 * D,), F32, parts=NB)
        bada = ar.alloc((6 * D,), F32, parts=NB)
        wa = [ar.alloc((8, 512), F32) for _ in range(2)]
        B_c, B_sc, B_scT, B_modtm, B_bada = Buf("c"), Buf("sc"), Buf("scT"), Buf("modtm"), Buf("bada")
        B_wa = [Buf("wa0"), Buf("wa1")]
        dma("sp", c_sb, c_d, "c3", writes=[B_c])
        dma("act", bada, bada_d, "c4", writes=[B_bada])
        S.op("act", lambda e: e.activation(out=sc_sb, in_=c_sb, func=AF.Silu), reads=[B_c], writes=[B_sc])

        def f_scT(e):
            for kc in range(8):
                i = e.transpose(out=ps[0][:, kc * NB:(kc + 1) * NB], in_=sc_sb[0:NB, kc * 128:(kc + 1) * 128],
                                identity=ident_f[0:NB, 0:NB])
            return i
        S.op("pe", f_scT, reads=[B_sc, B_const], writes=[PB[0]])
        S.op("dve", lambda e: e.tensor_copy(out=scT.rearrange("p a b -> p (a b)"), in_=ps[0][:, 0:8 * NB]),
             writes=[B_scT, PB[0]])
        wada_v = wada_d.rearrange("(kc p) n -> p kc n", p=128)
        for cb in range(12):
            w = wa[cb % 2]
            dma("sp" if cb % 2 == 0 else "act", w, wada_v[:, :, cb * 512:(cb + 1) * 512], f"wa{cb % 2}",
                writes=[B_wa[cb % 2]])
            pb = 1 + cb % 2

            def f_mod(e, w=w, pb=pb):
                for kc in range(8):
                    i = e.matmul(ps[pb][0:NB, :], lhsT=scT[:, kc, :], rhs=w[:, kc, :], start=(kc == 0), stop=(kc == 7))
                return i
            S.op("pe", f_mod, reads=[B_scT, B_wa[cb % 2]], writes=[PB[pb]])
            S.op("dve", lambda e, cb=cb, pb=pb: e.tensor_tensor(out=mod_tm[:, cb * 512:(cb + 1) * 512], in0=ps[pb][0:NB, :],
                                                               in1=bada[:, cb * 512:(cb + 1) * 512], op=ALU.add),
                 reads=[B_bada], writes=[B_modtm, PB[pb]])

        def f_modT(e):
            for j in range(48):
                i = e.transpose(out=ps[3][:, j * NB:(j + 1) * NB], in_=mod_tm[0:NB, j * 128:(j + 1) * 128],
                                identity=ident_f[0:NB, 0:NB])
            return i
        S.op("pe", f_modT, reads=[B_modtm, B_const], writes=[PB[3]])
        S.op("dve", lambda e: e.tensor_copy(out=modT.rearrange("p a b -> p (a b)"), in_=ps[3][:, 0:48 * NB]),
             writes=[B_mod, PB[3]])
        for (A_t, gcol, scj) in ((Am, PV_GMIX, 8), (Af, PV_GFFN, 32)):
            for b in range(NB):
                S.op("dve", lambda e, A_t=A_t, gcol=gcol, scj=scj, b=b: e.scalar_tensor_tensor(
                    out=A_t[:, :, b], in0=modT[:, scj:scj + 8, b], scalar=1.0, in1=pvec[:, gcol:gcol + 8],
                    op0=ALU.add, op1=ALU.mult), reads=[B_mod, B_const], writes=[B_mod])
        if "S" not in phases:
            emit_wconv()
        S.fence()
        ar.off = const_end

        def build_G(G_t, B_G, j0, b, dgs, B_dgs):
            for hf in range(2):
                for k4 in range(4):
                    kc = hf * 4 + k4
                    S.op("dve", lambda e, kc=kc, k4=k4: e.tensor_scalar(out=dgs[k4 % 2], in0=ident_f,
                                                                         scalar1=modT[:, j0 + kc, b:b + 1], scalar2=None,
                                                                         op0=ALU.mult),
                         reads=[B_mod, B_const], writes=[B_dgs[k4 % 2]])
                    S.op("pe", lambda e, k4=k4: e.matmul(ps[7][:, k4 * 128:(k4 + 1) * 128], lhsT=ones_f, rhs=dgs[k4 % 2],
                                                          start=True, stop=True),
                         reads=[B_dgs[k4 % 2], B_const], writes=[PB[7]])
                S.op("act", lambda e, hf=hf: e.copy(out=G_t[:, hf * 512:(hf + 1) * 512], in_=ps[7][:, :]),
                     writes=[B_G, PB[7]])


        def make_front(src_d, A_t, shj, tag, nxs=1, own_junk=False):
            xa = [ar.alloc((D,), F32) for _ in range(2)]
            xs = [ar.alloc((D,), BF16) for _ in range(nxs)]
            hT = ar.alloc((8, TB), BF16)
            ssq = ar.alloc((8,), F32)
            B_xa = [Buf(tag + "xa0"), Buf(tag + "xa1")]
            B_xs = [Buf(tag + "xs%d" % i) for i in range(nxs)]
            B_hTe, B_hTo = Buf(tag + "hTe"), Buf(tag + "hTo")
            B_st = [Buf(tag + "st%d" % i) for i in range(4)]
            if own_junk:
                junk = ar.alloc((D,), BF16)
                B_junk = Buf(tag + "junk")
            pv0 = ps[0][:, :].bitcast(BF16).rearrange("p (kc t) -> p kc t", kc=4)
            pv1 = ps[1][:, :].bitcast(BF16).rearrange("p (kc t) -> p kc t", kc=4)

            class F:
                pass

            def ld(tb, t):
                k = t % 2
                r0 = tb * TB + t * 128
                dma("sp", xa[k], src_d[r0:r0 + 128, :], tag + f"xa{k}", writes=[B_xa[k]])

            def st(tb, t):
                k = t % 2
                kx = t % nxs
                jo = junk if own_junk else xs[kx]
                jb_ = B_junk if own_junk else B_xs[kx]
                S.op("act", lambda e, k=k, t=t, jo=jo: e.activation(out=jo, in_=xa[k], func=AF.Square,
                                                                     accum_out=ssq[:, t:t + 1]),
                     reads=[B_xa[k]], writes=[jb_, B_st[t]])
                S.op("act", lambda e, t=t: e.activation(out=ssq[:, 4 + t:5 + t], in_=ssq[:, t:t + 1], func=AF.Ln,
                                                         scale=1.0 / D, bias=pvec[:, 127:128]),
                     reads=[B_st[t], B_const], writes=[B_st[t]])
                S.op("act", lambda e, t=t: e.activation(out=ssq[:, t:t + 1], in_=ssq[:, 4 + t:5 + t], func=AF.Exp,
                                                         scale=-0.5),
                     reads=[B_st[t]], writes=[B_st[t]])

            def xsop(tb, t):
                k = t % 2
                kx = t % nxs
                S.op("dve", lambda e, k=k, t=t, kx=kx: e.tensor_scalar(out=xs[kx], in0=xa[k], scalar1=ssq[:, t:t + 1],
                                                                       scalar2=None, op0=ALU.mult),
                     reads=[B_xa[k], B_st[t]], writes=[B_xs[kx]])

            def tr(tb, t):
                t2 = t % 2
                kx = t % nxs

                def f_tr(e, kx=kx, t2=t2):
                    for kc in range(8):
                        pv = pv0 if kc < 4 else pv1
                        i = e.transpose(out=pv[:, kc % 4, t2 * 128:(t2 + 1) * 128],
                                        in_=xs[kx][:, kc * 128:(kc + 1) * 128], identity=ident_b)
                    return i
                S.op("pe", f_tr, reads=[B_xs[kx], B_const], writes=[PB[0], PB[1]])

            def evac(tb, half):
                b = tb // BPS
                for kc in range(4):
                    S.op("act", lambda e, kc=kc, b=b, half=half: e.activation(
                        out=hT[:, kc, half * 256:(half + 1) * 256], in_=pv0[:, kc, :], func=AF.Identity,
                        scale=A_t[:, kc, b:b + 1], bias=modT[:, shj + kc, b:b + 1]),
                        reads=[B_mod], writes=[B_hTe, PB[0]])
                for kc in range(4, 8):
                    S.op("dve", lambda e, kc=kc, b=b, half=half: e.tensor_scalar(
                        out=hT[:, kc, half * 256:(half + 1) * 256], in0=pv1[:, kc - 4, :],
                        scalar1=A_t[:, kc, b:b + 1], scalar2=modT[:, shj + kc, b:b + 1],
                        op0=ALU.mult, op1=ALU.add),
                        reads=[B_mod], writes=[B_hTo, PB[1]])

            def pre(tb, t):
                ld(tb, t)
                st(tb, t)
                xsop(tb, t)

            def full(tb):
                for t in range(4):
                    pre(tb, t)
                    tr(tb, t)
                    if t % 2 == 1:
                        evac(tb, t // 2)
            F.ld, F.st, F.xsop, F.tr, F.evac, F.pre, F.full = ld, st, xsop, tr, evac, pre, full
            return F, hT, [B_hTe, B_hTo]

        def phase_S():
            import math
            ar.off = const_end
            I32 = mybir.dt.int32
            wu = ar.alloc((8, 512), BF16)
            B_wu = Buf("wu")
            for kc in range(8):
                dma("pool", wu[:, kc, :], win_d[kc * 128:(kc + 1) * 128, 1544:2056], "wu", writes=[B_wu])
            FR, hT, B_hT = make_front(x_d, Am, 0, "S", nxs=2, own_junk=True)
            uT = ar.alloc((4, TB), BF16)
            LagW = ar.alloc((4, 8, 128), BF16)
            W1 = ar.alloc((4, 2, 8, 128), BF16)
            W2 = ar.alloc((16, 8, 2, 32), BF16)
            WgluB = ar.alloc((4, 128), BF16)
            Mp = [ar.alloc((16, 64), F32) for _ in range(2)]
            Mm = [ar.alloc((16, 64), F32) for _ in range(2)]
            scanmask = ar.alloc((16, 64), F32)
            mu = [ar.alloc((16,), F32) for _ in range(2)]
            Hc = [ar.alloc((16,), F32) for _ in range(2)]
            tc_ = [ar.alloc((16,), F32) for _ in range(4)]
            tmp = [ar.alloc((16, 64), F32) for _ in range(2)]
            alias_start = ar.off
            Hinj = [ar.alloc((16, 64), F32) for _ in range(2)]
            Wm = [ar.alloc((16, 64), F32) for _ in range(2)]
            Zs = [ar.alloc((16, 64), F32) for _ in range(2)]
            Hb = [ar.alloc((16, 65), BF16) for _ in range(2)]
            yf = ar.alloc((4, TB), F32)
            Ylag = [ar.alloc((4, TB), F32) for _ in range(2)]
            zf2 = [ar.alloc((4, TB), F32) for _ in range(2)]
            zb = ar.alloc((4, TB), BF16)
            g1 = [ar.alloc((TB,), F32) for _ in range(2)]
            sg = [ar.alloc((TB,), F32) for _ in range(2)]
            sq = ar.alloc((4, TB), F32)
            rs = ar.alloc((TB,), F32)
            onb = ar.alloc((4, TB), BF16)
            run_end = ar.off
            print("phase S arena use (runtime)", ar.off)
            ar.off = alias_start
            s16 = ar.alloc((48,), F32)
            sB = ar.alloc((2, 16, 16), F32)
            sC = ar.alloc((2, 16, 16), F32)
            T = [ar.alloc((16,), F32) for _ in range(16)]
            Ti = ar.alloc((16,), F32)
            Pk = [ar.alloc((9, 16), F32) for _ in range(2)]
            gam = [ar.alloc((16,), F32) for _ in range(2)]
            Bb = [ar.alloc((16, 16), F32) for _ in range(2)]
            Xs = [ar.alloc((8, 16, 16), F32) for _ in range(2)]
            Xpad = [ar.alloc((8, 16, 2, 16), F32) for _ in range(2)]
            Cp = [ar.alloc((16, 2, 16), F32) for _ in range(2)]
            Et = [ar.alloc((16, 32), F32) for _ in range(2)]
            mask2 = ar.alloc((2,), F32)
            bmask = ar.alloc((128,), F32)
            mpow = [ar.alloc((16,), F32) for _ in range(2)]
            mpow2 = [ar.alloc((16,), F32) for _ in range(2)]
            nu = [ar.alloc((16,), F32) for _ in range(2)]
            print("phase S arena use (with setup)", ar.off)
            B_su = Buf("ssetup")

            def sop(eng, fn):
                S.op(eng, fn, reads=[B_su, B_const], writes=[B_su])

            def V(fn):
                sop("dve", fn)

            dma("sp", s16, ssm16_d, "c6", writes=[B_su])
            dma("sp", sB.rearrange("p a b c -> p (a b c)"), ssmB_d, "c7", writes=[B_su])
            dma("sp", sC.rearrange("p a b c -> p (a b c)"), ssmC_d, "c8", writes=[B_su])
            dma("pool", WgluB.rearrange("p a b -> p (a b)"), wglu_d, "c9", writes=[B_su])
            a_re, a_im, ldt = s16[:, 0:16], s16[:, 16:32], s16[:, 32:48]
            dt, ardt, th, er, us, uc, sn, cs, lbr, lbi, den, nr, t0, t1, t2, t3 = T
            sop("pool", lambda e: e.memset(mask2, 0.0))
            sop("pool", lambda e: e.memset(mask2[0:64, 0:1], 1.0))
            sop("pool", lambda e: e.memset(mask2[64:128, 1:2], 1.0))
            sop("pool", lambda e: e.memset(scanmask, 1.0))
            sop("pool", lambda e: e.memset(scanmask[:, :, 0:1], 0.0))
            bm3 = bmask.rearrange("p (a b) -> p a b", a=4)
            sop("pool", lambda e: e.memset(bmask, 0.0))
            sop("pool", lambda e: e.affine_select(out=bm3, in_=bm3, compare_op=ALU.is_gt, fill=1.0, base=1 - 32,
                                                  pattern=[[-32, 4], [0, 32]], channel_multiplier=1))
            sop("pool", lambda e: e.affine_select(out=bm3, in_=bm3, compare_op=ALU.is_ge, fill=0.0, base=0,
                                                  pattern=[[-32, 4], [0, 32]], channel_multiplier=1))
            sop("act", lambda e: e.activation(out=dt, in_=ldt, func=AF.Exp))
            V(lambda e: e.tensor_tensor(out=ardt, in0=a_re, in1=dt, op=ALU.mult))
            V(lambda e: e.tensor_tensor(out=th, in0=a_im, in1=dt, op=ALU.mult))
            sop("act", lambda e: e.activation(out=er, in_=ardt, func=AF.Exp))
            for (u_, shift) in ((us, 8.5), (uc, 8.75)):
                V(lambda e, u_=u_, shift=shift: e.tensor_scalar(out=u_, in0=th, scalar1=1.0 / (2 * math.pi), scalar2=shift,
                                                                op0=ALU.mult, op1=ALU.add))
                V(lambda e, u_=u_: e.tensor_copy(out=Ti.bitcast(I32), in_=u_))
                V(lambda e: e.tensor_copy(out=t0, in_=Ti.bitcast(I32)))
                V(lambda e, u_=u_: e.tensor_tensor(out=u_, in0=u_, in1=t0, op=ALU.subtract))
                V(lambda e, u_=u_: e.tensor_scalar(out=t0, in0=u_, scalar1=0.0, scalar2=None, op0=ALU.is_lt))
                V(lambda e, u_=u_: e.tensor_tensor(out=u_, in0=u_, in1=t0, op=ALU.add))
                V(lambda e, u_=u_: e.tensor_scalar(out=u_, in0=u_, scalar1=2 * math.pi, scalar2=math.pi,
                                                   op0=ALU.mult, op1=ALU.subtract))
            sop("act", lambda e: e.activation(out=sn, in_=us, func=AF.Sin))
            sop("act", lambda e: e.activation(out=cs, in_=uc, func=AF.Sin))
            V(lambda e: e.tensor_tensor(out=lbr, in0=er, in1=cs, op=ALU.mult))
            V(lambda e: e.tensor_tensor(out=lbi, in0=er, in1=sn, op=ALU.mult))
            V(lambda e: e.tensor_tensor(out=den, in0=a_re, in1=a_re, op=ALU.mult))
            V(lambda e: e.tensor_tensor(out=t0, in0=a_im, in1=a_im, op=ALU.mult))
            V(lambda e: e.tensor_tensor(out=den, in0=den, in1=t0, op=ALU.add))
            V(lambda e: e.reciprocal(out=den, in_=den))
            V(lambda e: e.tensor_scalar(out=nr, in0=lbr, scalar1=-1.0, scalar2=None, op0=ALU.add))
            V(lambda e: e.tensor_tensor(out=t0, in0=nr, in1=a_re, op=ALU.mult))
            V(lambda e: e.tensor_tensor(out=t1, in0=lbi, in1=a_im, op=ALU.mult))
            V(lambda e: e.tensor_tensor(out=t0, in0=t0, in1=t1, op=ALU.add))
            V(lambda e: e.tensor_tensor(out=gam[0], in0=t0, in1=den, op=ALU.mult))
            V(lambda e: e.tensor_tensor(out=t0, in0=lbi, in1=a_re, op=ALU.mult))
            V(lambda e: e.tensor_tensor(out=t1, in0=nr, in1=a_im, op=ALU.mult))
            V(lambda e: e.tensor_tensor(out=t0, in0=t0, in1=t1, op=ALU.subtract))
            V(lambda e: e.tensor_tensor(out=gam[1], in0=t0, in1=den, op=ALU.mult))

            def cmul(o_re, o_im, a_r, a_i, b_r, b_i, ta, tb_):
                V(lambda e: e.tensor_tensor(out=ta, in0=a_i, in1=b_i, op=ALU.mult))
                V(lambda e: e.tensor_tensor(out=o_re, in0=a_r, in1=b_r, op=ALU.mult))
                V(lambda e: e.tensor_tensor(out=o_re, in0=o_re, in1=ta, op=ALU.subtract))
                V(lambda e: e.tensor_tensor(out=tb_, in0=a_i, in1=b_r, op=ALU.mult))
                V(lambda e: e.tensor_tensor(out=o_im, in0=a_r, in1=b_i, op=ALU.mult))
                V(lambda e: e.tensor_tensor(out=o_im, in0=o_im, in1=tb_, op=ALU.add))

            V(lambda e: e.memset(Pk[0][:, 0, :], 1.0))
            V(lambda e: e.memset(Pk[1][:, 0, :], 0.0))
            for k in range(8):
                cmul(Pk[0][:, k + 1, :], Pk[1][:, k + 1, :], Pk[0][:, k, :], Pk[1][:, k, :], lbr, lbi, t2, t3)
            V(lambda e: e.tensor_copy(out=mu[0], in_=Pk[0][:, 8, :]))
            V(lambda e: e.tensor_copy(out=mu[1], in_=Pk[1][:, 8, :]))
            V(lambda e: e.tensor_tensor(out=t0, in0=mu[0], in1=mu[0], op=ALU.mult))
            V(lambda e: e.tensor_tensor(out=t1, in0=mu[1], in1=mu[1], op=ALU.mult))
            V(lambda e: e.tensor_tensor(out=t0, in0=t0, in1=t1, op=ALU.add))
            V(lambda e: e.reciprocal(out=t0, in_=t0))
            V(lambda e: e.tensor_tensor(out=nu[0], in0=mu[0], in1=t0, op=ALU.mult))
            V(lambda e: e.tensor_tensor(out=nu[1], in0=mu[1], in1=t0, op=ALU.mult))
            V(lambda e: e.tensor_scalar(out=nu[1], in0=nu[1], scalar1=-1.0, scalar2=None, op0=ALU.mult))
            for (Mt, base) in ((Mp, mu), (Mm, nu)):
                V(lambda e, Mt=Mt: e.memset(Mt[0][:, :, 0:1], 1.0))
                V(lambda e, Mt=Mt: e.memset(Mt[1][:, :, 0:1], 0.0))
                V(lambda e, base=base: e.tensor_copy(out=mpow[0], in_=base[0]))
                V(lambda e, base=base: e.tensor_copy(out=mpow[1], in_=base[1]))
                k = 1
                while k < 64:
                    br = mpow[0][:, :].unsqueeze(2).to_broadcast([128, 16, k])
                    bi = mpow[1][:, :].unsqueeze(2).to_broadcast([128, 16, k])
                    cmul(Mt[0][:, :, k:2 * k], Mt[1][:, :, k:2 * k], Mt[0][:, :, 0:k], Mt[1][:, :, 0:k], br, bi,
                         tmp[0][:, :, 0:k], tmp[1][:, :, 0:k])
                    if 2 * k < 64:
                        cmul(mpow2[0], mpow2[1], mpow[0], mpow[1], mpow[0], mpow[1], t2, t3)
                        V(lambda e: e.tensor_copy(out=mpow[0], in_=mpow2[0]))
                        V(lambda e: e.tensor_copy(out=mpow[1], in_=mpow2[1]))
                    k *= 2
            gbr = gam[0][:, :].unsqueeze(2).to_broadcast([128, 16, 16])
            gbi = gam[1][:, :].unsqueeze(2).to_broadcast([128, 16, 16])
            cmul(Bb[0], Bb[1], sB[:, 0], sB[:, 1], gbr, gbi, tmp[0][:, :, 0:16], tmp[1][:, :, 0:16])
            for tau in range(8):
                pr = Pk[0][:, tau, :].unsqueeze(2).to_broadcast([128, 16, 16])
                pi_ = Pk[1][:, tau, :].unsqueeze(2).to_broadcast([128, 16, 16])
                cmul(Xs[0][:, tau], Xs[1][:, tau], Bb[0], Bb[1], pr, pi_, tmp[0][:, :, 0:16], tmp[1][:, :, 0:16])
            for ri in range(2):
                for g2 in range(2):
                    V(lambda e, ri=ri, g2=g2: e.tensor_scalar(
                        out=Xpad[ri][:, :, :, g2, :].rearrange("p a b c -> p (a b) c"),
                        in0=Xs[ri].rearrange("p a b c -> p (a b) c"), scalar1=mask2[:, g2:g2 + 1], scalar2=None,
                        op0=ALU.mult))
                    V(lambda e, ri=ri, g2=g2: e.tensor_scalar(
                        out=Cp[ri][:, :, g2, :], in0=sC[:, ri], scalar1=mask2[:, g2:g2 + 1],
                        scalar2=(1.0 if ri == 0 else -1.0), op0=ALU.mult, op1=ALU.mult))
            for q in range(4):
                for tau in range(8):
                    def f_lag(e, q=q, tau=tau):
                        e.matmul(ps[2][:, 0:128], lhsT=Xpad[0][:, tau, 4 * q:4 * q + 4].rearrange("p a b c -> p (a b c)"),
                                 rhs=Cp[0][:, 4 * q:4 * q + 4].rearrange("p a b c -> p (a b c)"), start=True, stop=False)
                        return e.matmul(ps[2][:, 0:128], lhsT=Xpad[1][:, tau, 4 * q:4 * q + 4].rearrange("p a b c -> p (a b c)"),
                                        rhs=Cp[1][:, 4 * q:4 * q + 4].rearrange("p a b c -> p (a b c)"), start=False, stop=True)
                    S.op("pe", f_lag, reads=[B_su], writes=[PB[2]])
                    if tau == 0:
                        S.op("dve", lambda e: e.tensor_tensor(out=tmp[0].rearrange("p a b -> p (a b)")[:, 0:128], in0=ps[2][:, 0:128],
                                                              in1=bmask, op=ALU.mult), reads=[B_su], writes=[B_su, PB[2]])
                        V(lambda e, q=q: e.scalar_tensor_tensor(out=LagW[:, q, 0, :], in0=ident_f,
                                                                scalar=pvec[:, PV_DSK + q:PV_DSK + q + 1],
                                                                in1=tmp[0].rearrange("p a b -> p (a b)")[:, 0:128],
                                                                op0=ALU.mult, op1=ALU.add))
                    else:
                        S.op("dve", lambda e, q=q, tau=tau: e.tensor_tensor(out=LagW[:, q, tau, :], in0=ps[2][:, 0:128], in1=bmask,
                                                                            op=ALU.mult), reads=[B_su], writes=[B_su, PB[2]])
            for q in range(4):
                for ri in range(2):
                    for s in range(8):
                        S.op("pe", lambda e, q=q, ri=ri, s=s: e.transpose(
                            out=ps[3][:, 0:128], in_=Xpad[ri][:, 7 - s, 4 * q:4 * q + 4].rearrange("p a b c -> p (a b c)"),
                            identity=ident_f), reads=[B_su, B_const], writes=[PB[3]])
                        S.op("act", lambda e, q=q, ri=ri, s=s: e.copy(out=W1[:, q, ri, s, :], in_=ps[3][:, 0:128]),
                             reads=[B_su], writes=[B_su, PB[3]])
            for r in range(8):
                pr = Pk[0][:, r + 1, :].unsqueeze(2).to_broadcast([128, 16, 32])
                pi_ = Pk[1][:, r + 1, :].unsqueeze(2).to_broadcast([128, 16, 32])
                c0 = Cp[0].rearrange("p a b c -> p a (b c)")
                c1 = Cp[1].rearrange("p a b c -> p a (b c)")
                V(lambda e, pr=pr: e.tensor_tensor(out=Et[0], in0=c0, in1=pr, op=ALU.mult))
                V(lambda e, pi_=pi_: e.tensor_tensor(out=Et[1], in0=c1, in1=pi_, op=ALU.mult))
                V(lambda e, r=r: e.tensor_tensor(out=W2[:, :, r, 0, :], in0=Et[0], in1=Et[1], op=ALU.add))
                V(lambda e, pr=pr: e.tensor_tensor(out=Et[0], in0=c1, in1=pr, op=ALU.mult))
                V(lambda e, pi_=pi_: e.tensor_tensor(out=Et[1], in0=c0, in1=pi_, op=ALU.mult))
                V(lambda e, r=r: e.tensor_tensor(out=W2[:, :, r, 1, :], in0=Et[0], in1=Et[1], op=ALU.subtract))
            emit_wconv()
            S.fence()

            B_uT = Buf("uT")
            B_Hinj = [Buf("Hinj0"), Buf("Hinj1")]
            B_Wm = [Buf("Wm0"), Buf("Wm1")]
            B_Zs = [Buf("Zs0"), Buf("Zs1")]
            B_tmp = [Buf("tmp0"), Buf("tmp1")]
            B_Hb = [Buf("Hb0"), Buf("Hb1")]
            B_Hc = [Buf("Hc0"), Buf("Hc1")]
            B_tc = Buf("tc")
            B_yf = [Buf(f"yf{q}") for q in range(4)]
            B_zf2 = [[Buf(f"zf{k}_{q}") for q in range(4)] for k in range(2)]
            B_zb = [Buf(f"zb{q}") for q in range(4)]
            B_g1 = [Buf("g10"), Buf("g11")]
            B_sg = [Buf("sg0"), Buf("sg1")]
            B_sq = [Buf(f"sq{q}") for q in range(4)]
            B_rs = Buf("rs")
            B_onb = [Buf(f"onb{q}") for q in range(4)]
            uTv = uT.rearrange("p q (n r) -> p q n r", r=8)
            B_Yl = [[Buf(f"Yl{k}_{q}") for q in range(4)] for k in range(2)]

            def st_uproj(tb):
                for q in range(4):
                    pb = 2 + q % 2

                    def f_u(e, q=q, pb=pb):
                        for kc in range(8):
                            i = e.matmul(ps[pb][:, :], lhsT=wu[:, kc, q * 128:(q + 1) * 128], rhs=hT[:, kc, :],
                                         start=(kc == 0), stop=(kc == 7))
                        return i
                    S.op("pe", f_u, reads=[B_wu] + B_hT, writes=[PB[pb]])
                    S.op("act", lambda e, q=q, pb=pb: e.copy(out=uT[:, q, :], in_=ps[pb][:, :]), writes=[B_uT, PB[pb]])

            def st_lag(tb):
                kk = tb % 2
                for q in range(4):
                    pb = 4 + q % 2

                    def f_lagmm(e, q=q, pb=pb):
                        yv = ps[pb][:, :].rearrange("p (n r) -> p n r", r=8)
                        i = e.matmul(ps[pb][:, :], lhsT=LagW[:, q, 0, :], rhs=uT[:, q, :], start=True, stop=False,
                                     skip_group_check=True)
                        for tau in range(1, 8):
                            i = e.matmul(yv[:, :, tau:8], lhsT=LagW[:, q, tau, :], rhs=uTv[:, q, :, 0:8 - tau],
                                         start=False, stop=(tau == 7), skip_group_check=True)
                        return i
                    S.op("pe", f_lagmm, reads=[B_uT, B_su], writes=[PB[pb]])
                    S.op("act", lambda e, q=q, pb=pb, kk=kk: e.copy(out=Ylag[kk][:, q, :], in_=ps[pb][:, :]),
                         writes=[B_Yl[kk][q], PB[pb]])

            def st_inj(tb):
                for j in range(4):
                    def f_inj(e, j=j):
                        for q in range(4):
                            for ri in range(2):
                                c0 = (q * 2 + ri) * 64
                                for s in range(8):
                                    i = e.matmul(ps[j][:, c0:c0 + 64], lhsT=W1[32 * j:32 * j + 32, q, ri, s, :],
                                                 rhs=uTv[32 * j:32 * j + 32, q, :, s], start=(s == 0), stop=(s == 7),
                                                 tile_position=(32 * j, 0), skip_group_check=True)
                        return i
                    S.op("pe", f_inj, reads=[B_uT, B_su], writes=[PB[j]])
                    pjv = ps[j][:, :].rearrange("p (q r n) -> p q r n", q=4, r=2)
                    for ri in range(2):
                        hv = Hinj[ri].rearrange("p (q j) n -> p q j n", j=4)
                        if ri == 0:
                            S.op("act", lambda e, j=j, ri=ri, pjv=pjv, hv=hv: e.copy(out=hv[:, :, j, :], in_=pjv[:, :, ri, :]),
                                 writes=[B_Hinj[ri], PB[j]])
                        else:
                            S.op("dve", lambda e, j=j, ri=ri, pjv=pjv, hv=hv: e.tensor_copy(out=hv[:, :, j, :], in_=pjv[:, :, ri, :]),
                                 writes=[B_Hinj[ri], PB[j]])

            def st_scan_a(tb):
                lb = tb % BPS
                S.op("pool", lambda e: e.tensor_tensor(out=tmp[0], in0=Mm[1], in1=Hinj[1], op=ALU.mult),
                     reads=[B_Hinj[1], B_su], writes=[B_tmp[0]])
                S.op("pool", lambda e: e.tensor_tensor(out=Wm[0], in0=Mm[0], in1=Hinj[0], op=ALU.mult),
                     reads=[B_Hinj[0], B_su], writes=[B_Wm[0]])
                S.op("pool", lambda e: e.tensor_tensor(out=Wm[0], in0=Wm[0], in1=tmp[0], op=ALU.subtract),
                     reads=[B_tmp[0]], writes=[B_Wm[0]])
                S.op("pool", lambda e: e.tensor_tensor(out=tmp[1], in0=Mm[1], in1=Hinj[0], op=ALU.mult),
                     reads=[B_Hinj[0], B_su], writes=[B_tmp[1]])
                S.op("pool", lambda e: e.tensor_tensor(out=Wm[1], in0=Mm[0], in1=Hinj[1], op=ALU.mult),
                     reads=[B_Hinj[1], B_su], writes=[B_Wm[1]])
                S.op("pool", lambda e: e.tensor_tensor(out=Wm[1], in0=Wm[1], in1=tmp[1], op=ALU.add),
                     reads=[B_tmp[1]], writes=[B_Wm[1]])
                if lb > 0:
                    S.op("pool", lambda e: e.tensor_tensor(out=tc_[0], in0=mu[0], in1=Hc[0], op=ALU.mult),
                         reads=[B_Hc[0], B_su], writes=[B_tc])
                    S.op("pool", lambda e: e.tensor_tensor(out=tc_[1], in0=mu[1], in1=Hc[1], op=ALU.mult),
                         reads=[B_Hc[1], B_su], writes=[B_tc])
                    S.op("pool", lambda e: e.tensor_tensor(out=tc_[2], in0=mu[0], in1=Hc[1], op=ALU.mult),
                         reads=[B_Hc[1], B_su], writes=[B_tc])
                    S.op("pool", lambda e: e.tensor_tensor(out=tc_[3], in0=mu[1], in1=Hc[0], op=ALU.mult),
                         reads=[B_Hc[0], B_su], writes=[B_tc])
                    S.op("pool", lambda e: e.tensor_tensor(out=tc_[0], in0=tc_[0], in1=tc_[1], op=ALU.subtract),
                         reads=[B_tc], writes=[B_tc])
                    S.op("pool", lambda e: e.tensor_tensor(out=tc_[2], in0=tc_[2], in1=tc_[3], op=ALU.add),
                         reads=[B_tc], writes=[B_tc])
                    S.op("pool", lambda e: e.tensor_tensor(out=Wm[0][:, :, 0], in0=Wm[0][:, :, 0], in1=tc_[0], op=ALU.add),
                         reads=[B_tc, B_Wm[0]], writes=[B_Wm[0]])
                    S.op("pool", lambda e: e.tensor_tensor(out=Wm[1][:, :, 0], in0=Wm[1][:, :, 0], in1=tc_[2], op=ALU.add),
                         reads=[B_tc, B_Wm[1]], writes=[B_Wm[1]])
                for ri in range(2):
                    if lb == 0:
                        S.op("pool", lambda e, ri=ri: e.memset(Hb[ri][:, :, 0:1], 0.0), writes=[B_Hb[ri]])
                    else:
                        S.op("pool", lambda e, ri=ri: e.tensor_copy(out=Hb[ri][:, :, 0], in_=Hc[ri]),
                             reads=[B_Hc[ri]], writes=[B_Hb[ri]])

            def st_scan_b(tb):
                for ri in range(2):
                    S.op("dve", lambda e, ri=ri: e.tensor_tensor_scan(
                        out=Zs[ri].rearrange("p a b -> p (a b)"), data0=scanmask.rearrange("p a b -> p (a b)"),
                        data1=Wm[ri].rearrange("p a b -> p (a b)"), initial=0.0, op0=ALU.mult, op1=ALU.add),
                        reads=[B_Wm[ri], B_su], writes=[B_Zs[ri]])

            def st_scan_c(tb):
                S.op("pool", lambda e: e.tensor_tensor(out=tmp[0], in0=Mp[1], in1=Zs[1], op=ALU.mult),
                     reads=[B_Zs[1], B_su], writes=[B_tmp[0]])
                S.op("pool", lambda e: e.tensor_tensor(out=Wm[0], in0=Mp[0], in1=Zs[0], op=ALU.mult),
                     reads=[B_Zs[0], B_su], writes=[B_Wm[0]])
                S.op("pool", lambda e: e.tensor_tensor(out=Wm[0], in0=Wm[0], in1=tmp[0], op=ALU.subtract),
                     reads=[B_tmp[0]], writes=[B_Wm[0]])
                S.op("pool", lambda e: e.tensor_tensor(out=tmp[1], in0=Mp[1], in1=Zs[0], op=ALU.mult),
                     reads=[B_Zs[0], B_su], writes=[B_tmp[1]])
                S.op("pool", lambda e: e.tensor_tensor(out=Wm[1], in0=Mp[0], in1=Zs[1], op=ALU.mult),
                     reads=[B_Zs[1], B_su], writes=[B_Wm[1]])
                S.op("pool", lambda e: e.tensor_tensor(out=Wm[1], in0=Wm[1], in1=tmp[1], op=ALU.add),
                     reads=[B_tmp[1]], writes=[B_Wm[1]])
                for ri in range(2):
                    S.op("pool", lambda e, ri=ri: e.tensor_copy(out=Hb[ri][:, :, 1:65], in_=Wm[ri]), reads=[B_Wm[ri]], writes=[B_Hb[ri]])
                    S.op("pool", lambda e, ri=ri: e.tensor_copy(out=Hc[ri], in_=Wm[ri][:, :, 63]), reads=[B_Wm[ri]], writes=[B_Hc[ri]])

            def st_yinter_c1(tb):
                kk = tb % 2
                zf = zf2[kk]
                B_zf = B_zf2[kk]
                for q in range(4):
                    pb = 6 + q % 2

                    def f_yi(e, q=q, pb=pb):
                        yv = ps[pb][:, :].rearrange("p (n r) -> p n r", r=8)
                        for j in range(4):
                            pi_ = 4 * q + j
                            for r in range(8):
                                for ri in range(2):
                                    last = (j == 3 and r == 7 and ri == 1)
                                    i = e.matmul(yv[32 * j:32 * j + 32, :, r], lhsT=W2[:, pi_, r, ri, :], rhs=Hb[ri][:, pi_, 0:64],
                                                 start=(r == 0 and ri == 0), stop=last, tile_position=(0, 32 * j),
                                                 skip_group_check=True)
                        return i
                    S.op("pe", f_yi, reads=[B_Hb[0], B_Hb[1], B_su], writes=[PB[pb]])
                    k = q % 2
                    S.op("dve", lambda e, q=q, pb=pb, kk=kk: e.tensor_tensor(out=yf[:, q, :], in0=ps[pb][:, :], in1=Ylag[kk][:, q, :],
                                                                            op=ALU.add),
                         reads=[B_Yl[kk][q]], writes=[B_yf[q], PB[pb]])
                    S.op("dve", lambda e, q=q, k=k: e.tensor_tensor(out=g1[k], in0=yf[:, q, :], in1=yf[:, q, :], op=ALU.mult),
                         reads=[B_yf[q]], writes=[B_g1[k]])
                    S.op("dve", lambda e, k=k: e.tensor_scalar(out=g1[k], in0=g1[k], scalar1=0.044715, scalar2=1.0,
                                                               op0=ALU.mult, op1=ALU.add), reads=[B_g1[k]], writes=[B_g1[k]])
                    S.op("dve", lambda e, q=q, k=k: e.tensor_tensor(out=g1[k], in0=g1[k], in1=yf[:, q, :], op=ALU.mult),
                         reads=[B_g1[k], B_yf[q]], writes=[B_g1[k]])
                    S.op("act", lambda e, k=k: e.activation(out=sg[k], in_=g1[k], func=AF.Sigmoid, scale=1.5957691216057308),
                         reads=[B_g1[k]], writes=[B_sg[k]])
                    S.op("dve", lambda e, q=q, k=k: e.tensor_tensor(out=zf[:, q, :], in0=yf[:, q, :], in1=sg[k], op=ALU.mult),
                         reads=[B_yf[q], B_sg[k]], writes=[B_zf[q]])
                    S.op("act", lambda e, q=q: e.copy(out=zb[:, q, :], in_=zf[:, q, :]), reads=[B_zf[q]], writes=[B_zb[q]])

            def st_c2a(tb):
                zf = zf2[tb % 2]
                B_zf = B_zf2[tb % 2]
                for q in range(4):
                    k = q % 2
                    pb = 4 + k
                    S.op("pe", lambda e, q=q, pb=pb: e.matmul(ps[pb][:, :], lhsT=WgluB[:, q, :], rhs=zb[:, q, :], start=True, stop=True),
                         reads=[B_zb[q], B_su], writes=[PB[pb]])
                    S.op("act", lambda e, q=q, k=k, pb=pb: e.activation(out=sg[k], in_=ps[pb][:, :], func=AF.Sigmoid,
                                                                        bias=pvec[:, PV_BGLU + q:PV_BGLU + q + 1], scale=1.0),
                         reads=[B_const], writes=[B_sg[k], PB[pb]])
                    S.op("dve", lambda e, q=q, k=k: e.tensor_tensor(out=zf[:, q, :], in0=zf[:, q, :], in1=sg[k], op=ALU.mult),
                         reads=[B_sg[k], B_zf[q]], writes=[B_zf[q]])
                    S.op("act", lambda e, q=q: e.activation(out=sq[:, q, :], in_=zf[:, q, :], func=AF.Square),
                         reads=[B_zf[q]], writes=[B_sq[q]])

            def st_c2b(tb):
                zf = zf2[tb % 2]
                B_zf = B_zf2[tb % 2]

                def f_ss(e):
                    for q in range(4):
                        i = e.matmul(ps[4][:, :], lhsT=ones_f, rhs=sq[:, q, :], start=(q == 0), stop=(q == 3))
                    return i
                S.op("pe", f_ss, reads=B_sq + [B_const], writes=[PB[4]])
                S.op("act", lambda e: e.activation(out=rs, in_=ps[4][:, :], func=AF.Ln, scale=1.0 / 512, bias=pvec[:, 127:128]),
                     reads=[B_const], writes=[B_rs, PB[4]])
                S.op("act", lambda e: e.activation(out=rs, in_=rs, func=AF.Exp, scale=-0.5), reads=[B_rs], writes=[B_rs])
                for q in range(4):
                    S.op("dve", lambda e, q=q: e.scalar_tensor_tensor(
                        out=onb[:, q, :], in0=zf[:, q, :], scalar=pvec[:, PV_GCAT + 4 + q:PV_GCAT + 5 + q], in1=rs,
                        op0=ALU.mult, op1=ALU.mult), reads=[B_zf[q], B_rs, B_const], writes=[B_onb[q]])
                    dma("sp", ssmT_d[q, :, tb * TB:(tb + 1) * TB], onb[:, q, :], f"so{q}", reads=[B_onb[q]], writes=[B_ssmT])
                    if "A" not in phases:
                        dma("act", dbg_d["zf"][q, :, tb * TB:(tb + 1) * TB], zf[:, q, :], f"dz{q}", reads=[B_zf[q]])

            FR.full(0)
            st_uproj(0)
            if NBLK > 1:
                FR.full(1)
            st_lag(0)
            st_inj(0)
            if NBLK > 2:
                FR.ld(2, 0)
                FR.ld(2, 1)
            for tb in range(NBLK):
                nf = tb + 2 < NBLK
                if nf:
                    FR.st(tb + 2, 0)
                    FR.xsop(tb + 2, 0)
                    FR.st(tb + 2, 1)
                    FR.xsop(tb + 2, 1)
                if tb + 1 < NBLK:
                    st_uproj(tb + 1)
                st_scan_a(tb)
                if nf:
                    FR.tr(tb + 2, 0)
                    FR.pre(tb + 2, 2)
                    FR.tr(tb + 2, 1)
                    FR.evac(tb + 2, 0)
                    FR.pre(tb + 2, 3)
                st_scan_b(tb)
                if tb + 1 < NBLK:
                    st_lag(tb + 1)
                st_scan_c(tb)
                if nf:
                    FR.tr(tb + 2, 2)
                    FR.tr(tb + 2, 3)
                    FR.evac(tb + 2, 1)
                if tb + 1 < NBLK:
                    st_inj(tb + 1)
                if tb >= 1:
                    st_c2a(tb - 1)
                if tb + 3 < NBLK:
                    FR.ld(tb + 3, 0)
                    FR.ld(tb + 3, 1)
                st_yinter_c1(tb)
                if tb >= 1:
                    st_c2b(tb - 1)
            st_c2a(NBLK - 1)
            st_c2b(NBLK - 1)
            S.fence()

        def phase_A():
            ar.off = const_end
            dstA = x1_d if "B" in phases else y_d
            win = ar.alloc((8, INC), BF16)
            wout = ar.alloc((8, D), BF16)
            B_win, B_wout = Buf("win"), Buf("wout")
            winb_v = winb_d.rearrange("(kc p) n -> p kc n", p=128)
            for kc in range(0, 8, 2):
                dma("sp" if (kc // 2) % 2 == 0 else "act", win[:, kc:kc + 2, :], winb_v[:, kc:kc + 2, :], "win",
                    reads=[B_wcv], writes=[B_win])
            woutb_v = woutb_d.rearrange("(kc p) n -> p kc n", p=128)
            dma("sp", wout[:, 0:4, :], woutb_v[:, 0:4, :], "wout", reads=[B_wcv], writes=[B_wout])
            dma("act", wout[:, 4:8, :], woutb_v[:, 4:8, :], "wout", reads=[B_wcv], writes=[B_wout])
            FR, hT, B_hT = make_front(x_d, Am, 0, "A", nxs=2, own_junk=True)
            KT = ar.alloc((4, SEQ), BF16)
            Vp = ar.alloc((32, 8, 65), BF16)
            Fcol = ar.alloc((32, 8), F32)
            QT = [ar.alloc((4, TB), BF16) for _ in range(2)]
            NPT = 6
            PT = [ar.alloc((256,), BF16) for _ in range(NPT)]
            attn_tm = ar.alloc((4, 512), F32)
            attn_n = ar.alloc((512,), BF16)
            mixT = ar.alloc((8, TB), BF16)
            xr = [ar.alloc((D,), F32) for _ in range(2)]
            ot1 = ar.alloc((D,), F32)
            Gm = ar.alloc((D,), F32)
            fe = ar.alloc((TB,), F32, parts=8)
            fl = ar.alloc((TB,), F32, parts=8)
            Fblk = ar.alloc((TB,), F32, parts=8)
            onesrow = ar.alloc((TB,), F32, parts=8)
            carryF = ar.alloc((8,), F32, parts=8)
            negb = ar.alloc((8,), F32, parts=8)
            Fmid = ar.alloc((2, 8), F32)
            NBT = 4
            biasT = [ar.alloc((32,), F32) for _ in range(NBT)]
            trimask = ar.alloc((128,), BF16)
            rec = [ar.alloc((8,), F32) for _ in range(2)]
            stA = ar.alloc((8,), F32)
            print("phase A arena use", ar.off)
            B_KT = [[Buf(f"KT{j}_{l}") for l in range(BPS)] for j in range(4)]
            B_Vp = [Buf(f"Vp{l}") for l in range(BPS)]
            B_Fcol = [Buf(f"Fcol{l}") for l in range(BPS)]
            B_QT = [[Buf(f"QT{k}_{j}") for j in range(4)] for k in range(2)]
            B_PT = [Buf(f"PT{i}") for i in range(NPT)]
            B_attn = [Buf(f"attn{t}") for t in range(4)]
            B_attn_n = Buf("attn_n")
            B_mixa, B_mixs = Buf("mixa"), Buf("mixs")
            B_xr = [Buf("Axr0"), Buf("Axr1")]
            B_ot1 = Buf("Aot")
            B_Gm = Buf("Gm")
            B_fe, B_fl, B_Fblk, B_cF = Buf("fe"), Buf("fl"), Buf("Fblk"), Buf("carryF")
            B_Fmid = Buf("Fmid")
            B_bias = [Buf(f"bias{i}") for i in range(NBT)]
            B_rec = [Buf("rec0"), Buf("rec1")]
            B_stA = [Buf(f"stA{t}") for t in range(4)]
            B_Aconst = Buf("Aconst")
            dgs = [attn_tm[:, 0, 0:128], attn_tm[:, 1, 0:128]]
            B_dgs = [B_attn[0], B_attn[1]]

            S.op("pool", lambda e: e.memset(trimask, 1.0), writes=[B_Aconst])
            S.op("pool", lambda e: e.affine_select(out=trimask, in_=trimask, compare_op=ALU.is_ge, fill=0.0, base=0,
                                                   pattern=[[1, 128]], channel_multiplier=-1),
                 reads=[B_Aconst], writes=[B_Aconst])
            S.op("pool", lambda e: e.memset(onesrow, 1.0), writes=[B_Aconst])
            S.op("pool", lambda e: e.memset(Vp[:, :, :, 64:65], 1.0), writes=[B_Aconst])
            dma("sp", negb[:, 0:1], bfg_d, "c5", writes=[B_Aconst])
            S.op("dve", lambda e: e.tensor_scalar(out=negb, in0=negb, scalar1=-1.0, scalar2=None, op0=ALU.mult),
                 reads=[B_Aconst], writes=[B_Aconst])
            if "S" not in phases:
                S.op("pool", lambda e: e.memset(mixT[:, 4:8, :], 0.0), writes=[B_mixs])

            unit = 0
            cntx = 0
            nbias = 0
            def do_proj(tb):
                b = tb // BPS
                lb = tb % BPS
                qk = tb % 2
                pcnt = 0
                def f_f(e):
                    for kc in range(8):
                        i = e.matmul(ps[7][0:8, :], lhsT=win[:, kc, 1536:1544], rhs=hT[:, kc, :], start=(kc == 0), stop=(kc == 7))
                    return i
                S.op("pe", f_f, reads=[B_win] + B_hT, writes=[PB[7]])
                S.op("act", lambda e: e.activation(out=fe, in_=ps[7][0:8, :], func=AF.Exp, scale=-1.0, bias=negb[:, 0:1]),
                     reads=[B_Aconst], writes=[B_fe, PB[7]])
                S.op("act", lambda e: e.activation(out=fl, in_=fe, func=AF.Ln, bias=1.0, scale=1.0),
                     reads=[B_fe], writes=[B_fl])
                if lb == 0:
                    S.op("dve", lambda e: e.tensor_tensor_scan(out=Fblk, data0=onesrow, data1=fl, initial=0.0,
                                                               op0=ALU.mult, op1=ALU.subtract),
                         reads=[B_fl, B_Aconst], writes=[B_Fblk])
                else:
                    S.op("dve", lambda e: e.tensor_tensor_scan(out=Fblk, data0=onesrow, data1=fl, initial=carryF[:, 0:1],
                                                               op0=ALU.mult, op1=ALU.subtract),
                         reads=[B_fl, B_Aconst, B_cF], writes=[B_Fblk])
                S.op("dve", lambda e: e.tensor_copy(out=carryF[:, 0:1], in_=Fblk[:, TB - 1:TB]), reads=[B_Fblk], writes=[B_cF])

                for j in range(4):
                    pb = 2 + pcnt % 2
                    pcnt += 1

                    def f_k(e, j=j, pb=pb):
                        for kc in range(8):
                            i = e.matmul(ps[pb][:, :], lhsT=win[:, kc, 512 + j * 128:512 + (j + 1) * 128], rhs=hT[:, kc, :],
                                         start=(kc == 0), stop=(kc == 7))
                        return i
                    S.op("pe", f_k, reads=[B_win] + B_hT, writes=[PB[pb]])
                    S.op("dve", lambda e, j=j, pb=pb, lb=lb: e.tensor_copy(out=KT[:, j, lb * TB:(lb + 1) * TB], in_=ps[pb][:, :]),
                         writes=[B_KT[j][lb], PB[pb]])
                def f_ft(e):
                    for t in range(4):
                        i = e.transpose(out=ps[4][:, t * 8:(t + 1) * 8], in_=Fblk[0:8, t * 128:(t + 1) * 128],
                                        identity=ident_f[0:8, 0:8])
                    return i
                S.op("pe", f_ft, reads=[B_Fblk, B_const], writes=[PB[4]])
                S.op("dve", lambda e, lb=lb: e.tensor_copy(out=Fcol[:, 4 * lb:4 * lb + 4, :].rearrange("p a b -> p (a b)"),
                                                          in_=ps[4][:, 0:32]), writes=[B_Fcol[lb], PB[4]])

                for j in range(4):
                    pb = 2 + pcnt % 2
                    pcnt += 1

                    def f_q(e, j=j, pb=pb):
                        for kc in range(8):
                            i = e.matmul(ps[pb][:, :], lhsT=win[:, kc, j * 128:(j + 1) * 128], rhs=hT[:, kc, :],
                                         start=(kc == 0), stop=(kc == 7))
                        return i
                    S.op("pe", f_q, reads=[B_win] + B_hT, writes=[PB[pb]])
                    S.op("dve", lambda e, j=j, pb=pb, qk=qk: e.tensor_copy(out=QT[qk][:, j, :], in_=ps[pb][:, :]),
                         writes=[B_QT[qk][j], PB[pb]])
                def f_fm(e, lb=lb):
                    for c in range(2):
                        i = e.matmul(ps[4][:, 64 + c * 8:64 + (c + 1) * 8], lhsT=ones_f[0:1, :],
                                     rhs=Fcol[0:1, 4 * lb + 2 * c + 1, :], start=True, stop=True)
                    return i
                S.op("pe", f_fm, reads=[B_Fcol[lb], B_const], writes=[PB[4]])
                S.op("dve", lambda e: e.tensor_copy(out=Fmid.rearrange("p a b -> p (a b)"), in_=ps[4][:, 64:80]),
                     writes=[B_Fmid, PB[4]])

                for t in range(4):
                    pb = 2 + pcnt % 2
                    pcnt += 1

                    def f_v(e, t=t, pb=pb):
                        for kc in range(8):
                            i = e.matmul(ps[pb][:, :], lhsT=hT[:, kc, t * 128:(t + 1) * 128], rhs=win[:, kc, 1024:1536],
                                         start=(kc == 0), stop=(kc == 7))
                        return i
                    S.op("pe", f_v, reads=[B_win] + B_hT, writes=[PB[pb]])
                    S.op("dve", lambda e, t=t, pb=pb, lb=lb: e.tensor_copy(
                        out=Vp[:, 4 * lb + t, :, 0:64], in_=ps[pb][:, :].rearrange("p (h d) -> p h d", h=8)),
                        writes=[B_Vp[lb], PB[pb]])
            def do_attn(tb):
                nonlocal unit, nbias
                b = tb // BPS
                lb = tb % BPS
                qk = tb % 2
                units = []
                for c in range(2):
                    i2 = 2 * lb + c
                    nj = 2 * i2 + 2
                    for jp in range(4):
                        for jb in range(nj):
                            for hh in range(2):
                                units.append((c, 2 * jp + hh, jb, nj))
                LOOK = 2
                ust = {}

                def emit_st(ui, c, h, jb, nj):
                    nonlocal unit, nbias
                    j = h // 2
                    r0 = 64 * (h % 2)
                    if jb == 0:
                        bi = nbias % NBT
                        nbias += 1
                        ust[(c, h)] = bi
                        S.op("dve", lambda e, bi=bi, nj=nj, h=h, c=c: e.tensor_scalar(
                            out=biasT[bi][:, 0:nj], in0=Fcol[:, 0:nj, h], scalar1=-1.0, scalar2=Fmid[:, c, h:h + 1],
                            op0=ALU.mult, op1=ALU.add),
                            reads=[B_Fcol[l] for l in range(lb + 1)] + [B_Fmid], writes=[B_bias[bi]])
                    bi = ust[(c, h)]
                    sb = (2, 3, 4, 7, 0, 1)[unit % 6]
                    pi = unit % NPT
                    unit += 1
                    diag1 = (jb == nj - 1)
                    diag0 = (jb == nj - 2)
                    q0 = c * 256 + (128 if diag1 else 0)
                    ncol = 128 if diag1 else 256
                    klb = jb // 4
                    S.op("pe", lambda e, sb=sb, r0=r0, j=j, jb=jb, q0=q0, ncol=ncol, qk=qk: e.matmul(
                        ps[sb][:, 0:ncol], lhsT=KT[r0:r0 + 64, j, jb * 128:(jb + 1) * 128],
                        rhs=QT[qk][r0:r0 + 64, j, q0:q0 + ncol], start=True, stop=True),
                        reads=[B_KT[j][klb], B_QT[qk][j]], writes=[PB[sb]])
                    S.op("act", lambda e, sb=sb, pi=pi, ncol=ncol, bi=bi, jb=jb: e.activation(
                        out=PT[pi][:, 0:ncol], in_=ps[sb][:, 0:ncol], func=AF.Exp, scale=0.125,
                        bias=biasT[bi][:, jb:jb + 1]),
                        reads=[B_bias[bi]], writes=[B_PT[pi], PB[sb]])
                    if diag0 or diag1:
                        S.op("dve", lambda e, pi=pi: e.tensor_tensor(out=PT[pi][:, 0:128], in0=PT[pi][:, 0:128],
                                                                     in1=trimask, op=ALU.mult),
                             reads=[B_Aconst], writes=[B_PT[pi]])
                    return pi

                def emit_pv(pi, c, h, jb, nj):
                    ab = 5 + h % 2
                    diag1 = (jb == nj - 1)
                    klb = jb // 4

                    def f_pv(e, pi=pi, jb=jb, h=h, ab=ab, diag1=diag1, nj=nj):
                        if diag1:
                            i = e.matmul(ps[ab][:, 128:193], lhsT=PT[pi][:, 0:128], rhs=Vp[:, jb, h, :],
                                         start=False, stop=True, skip_group_check=True)
                        else:
                            e.matmul(ps[ab][:, 0:65], lhsT=PT[pi][:, 0:128], rhs=Vp[:, jb, h, :],
                                     start=(jb == 0), stop=(jb == nj - 2), skip_group_check=True)
                            i = e.matmul(ps[ab][:, 128:193], lhsT=PT[pi][:, 128:256], rhs=Vp[:, jb, h, :],
                                         start=False, stop=False, skip_group_check=True)
                        return i
                    S.op("pe", f_pv, reads=[B_PT[pi], B_Vp[klb], B_Aconst], writes=[PB[ab]])
                    if jb == nj - 1:
                        rk = h % 2
                        accv = ps[ab][:, 0:256].rearrange("p (c d) -> p c d", c=2)
                        S.op("dve", lambda e, rk=rk, accv=accv: e.reciprocal(out=rec[rk][:, 0:2], in_=accv[:, :, 64]),
                             writes=[B_rec[rk], PB[ab]])
                        for cc in range(2):
                            t = 2 * c + cc
                            S.op("dve", lambda e, rk=rk, accv=accv, cc=cc, t=t, h=h: e.tensor_scalar(
                                out=attn_tm[:, t, h * 64:(h + 1) * 64], in0=accv[:, cc, 0:64], scalar1=rec[rk][:, cc:cc + 1],
                                scalar2=None, op0=ALU.mult),
                                reads=[B_rec[rk]], writes=[B_attn[t], PB[ab]])

                pis = []
                nstep = len(units) // 2
                LK = 2
                for st in range(nstep + LK):
                    if st < nstep:
                        pis.append(emit_st(2 * st, *units[2 * st]))
                        pis.append(emit_st(2 * st + 1, *units[2 * st + 1]))
                    if st >= LK:
                        s0 = st - LK
                        emit_pv(pis[2 * s0], *units[2 * s0])
                        emit_pv(pis[2 * s0 + 1], *units[2 * s0 + 1])
            def do_out(tb):
                nonlocal cntx
                b = tb // BPS
                lb = tb % BPS
                qk = tb % 2
                pv0 = ps[0][:, :].bitcast(BF16).rearrange("p (kc t) -> p kc t", kc=4)
                for half in range(2):
                    for t2 in range(2):
                        t = half * 2 + t2
                        S.op("act", lambda e, t=t: e.activation(out=ot1[:, 0:512], in_=attn_tm[:, t, :], func=AF.Square,
                                                                 accum_out=stA[:, t:t + 1]),
                             reads=[B_attn[t]], writes=[B_ot1, B_stA[t]])
                        S.op("act", lambda e, t=t: e.activation(out=stA[:, 4 + t:5 + t], in_=stA[:, t:t + 1], func=AF.Ln,
                                                                 scale=1.0 / 512, bias=pvec[:, 127:128]),
                             reads=[B_stA[t], B_const], writes=[B_stA[t]])
                        S.op("act", lambda e, t=t: e.activation(out=stA[:, t:t + 1], in_=stA[:, 4 + t:5 + t], func=AF.Exp,
                                                                 scale=-0.5), reads=[B_stA[t]], writes=[B_stA[t]])
                        S.op("dve", lambda e, t=t: e.tensor_scalar(out=attn_n, in0=attn_tm[:, t, :], scalar1=stA[:, t:t + 1],
                                                                   scalar2=None, op0=ALU.mult),
                             reads=[B_attn[t], B_stA[t]], writes=[B_attn_n])

                        def f_tr2(e, t2=t2):
                            for kc in range(4):
                                i = e.transpose(out=pv0[:, kc, t2 * 128:(t2 + 1) * 128], in_=attn_n[:, kc * 128:(kc + 1) * 128],
                                                identity=ident_b)
                            return i
                        S.op("pe", f_tr2, reads=[B_attn_n, B_const], writes=[PB[0]])
                    for kc in range(4):
                        S.op("dve", lambda e, kc=kc, half=half: e.tensor_scalar(
                            out=mixT[:, kc, half * 256:(half + 1) * 256], in0=pv0[:, kc, :],
                            scalar1=pvec[:, PV_GCAT + kc:PV_GCAT + kc + 1], scalar2=None, op0=ALU.mult),
                            reads=[B_const], writes=[B_mixa, PB[0]])
                if "S" in phases:
                    for q in range(4):
                        dma("pool", mixT[:, 4 + q, :], ssmT_d[q, :, tb * TB:(tb + 1) * TB], "mixs", writes=[B_mixs])
                for t in range(4):
                    k = cntx % 2
                    cntx += 1
                    rr = tb * TB + t * 128
                    dma("pool", xr[k], x_d[rr:rr + 128, :], f"Axr{k}", writes=[B_xr[k]])
                    for nb in range(2):
                        pb = 2 + nb

                        def f_o(e, t=t, nb=nb, pb=pb):
                            for kc in range(8):
                                i = e.matmul(ps[pb][:, :], lhsT=mixT[:, kc, t * 128:(t + 1) * 128],
                                             rhs=wout[:, kc, nb * 512:(nb + 1) * 512], start=(kc == 0), stop=(kc == 7))
                            return i
                        S.op("pe", f_o, reads=[B_mixa, B_mixs, B_wout], writes=[PB[pb]])
                        S.op("dve", lambda e, nb=nb, pb=pb: e.tensor_tensor(
                            out=ot1[:, nb * 512:(nb + 1) * 512], in0=ps[pb][:, :], in1=Gm[:, nb * 512:(nb + 1) * 512],
                            op=ALU.mult), reads=[B_Gm], writes=[B_ot1, PB[pb]])
                    S.op("dve", lambda e, k=k: e.tensor_tensor(out=xr[k], in0=ot1, in1=xr[k], op=ALU.add),
                         reads=[B_ot1], writes=[B_xr[k]])
                    dma("sp", dstA[rr:rr + 128, :], xr[k], f"Axo{k}", reads=[B_xr[k]], writes=[B_x1s])
            FR.full(0)
            do_proj(0)
            FR.full(1)
            for tb in range(NBLK):
                if tb % BPS == 0:
                    build_G(Gm, B_Gm, 16, tb // BPS, dgs, B_dgs)
                do_attn(tb)
                nf = tb + 2 < NBLK
                if nf:
                    FR.pre(tb + 2, 0)
                    FR.pre(tb + 2, 1)
                if tb + 1 < NBLK:
                    do_proj(tb + 1)
                if nf:
                    FR.tr(tb + 2, 0)
                    FR.pre(tb + 2, 2)
                    FR.tr(tb + 2, 1)
                    FR.evac(tb + 2, 0)
                    FR.pre(tb + 2, 3)
                do_out(tb)
                if nf:
                    FR.tr(tb + 2, 2)
                    FR.tr(tb + 2, 3)
                    FR.evac(tb + 2, 1)
            S.fence()

        def phase_B():
            ar.off = const_end
            srcB = x1_d if "A" in phases else x_d
            wup = ar.alloc((8, 2 * DFF), BF16)
            wdn = ar.alloc((NFC, D), BF16)
            B_wup, B_wdn = Buf("wup"), Buf("wdn")
            wupb_v = wupb_d.rearrange("(kc p) n -> p kc n", p=128)
            for kc in range(8):
                dma("sp" if kc % 2 == 0 else "act", wup[:, kc, :], wupb_v[:, kc, :], "wup", reads=[B_wcv], writes=[B_wup])
            wdnb_v = wdnb_d.rearrange("(fc p) n -> p fc n", p=128)
            for i_, g0 in enumerate(range(0, NFC, 6)):
                g1_ = min(NFC, g0 + 6)
                dma("sp" if i_ % 2 == 0 else "act", wdn[:, g0:g1_, :], wdnb_v[:, g0:g1_, :], "wdn", reads=[B_wcv], writes=[B_wdn])
            FR, hT, B_hT = make_front(srcB, Af, 24, "B")
            actT = ar.alloc((NFC, TB), BF16)
            B_actT = Buf("actT")
            gsb = [ar.alloc((TB + 2,), F32) for _ in range(2)]
            ct1 = [ar.alloc((TB,), F32) for _ in range(2)]
            ct2 = [ar.alloc((TB,), F32) for _ in range(2)]
            B_gsb = [Buf("gsb0"), Buf("gsb1")]
            B_ct1 = [Buf("ct10"), Buf("ct11")]
            B_ct2 = [Buf("ct20"), Buf("ct21")]
            carry = ar.alloc((NFC, 2), F32)
            B_carry = [Buf("carry%d" % i) for i in range(NFC)]
            xr = [ar.alloc((D,), F32) for _ in range(2)]
            ot1 = ar.alloc((D,), F32)
            st2 = ar.alloc((8,), F32)
            Gf = ar.alloc((D,), F32)
            dgs = [gsb[0][:, 0:128], gsb[1][:, 0:128]]
            B_dgs = B_gsb
            B_Gf = Buf("Gf")
            B_xr = [Buf("xr0"), Buf("xr1")]
            B_ot1 = Buf("ot")
            B_st2 = [Buf("st20"), Buf("st21")]
            print("phase B arena use", ar.off)
            cnt2 = 0
            pend_tail = [None]
            for tb in range(NBLK):
                b = tb // BPS
                lb = tb % BPS
                if lb == 0:
                    build_G(Gf, B_Gf, 40, b, dgs, B_dgs)
                if tb == 0:
                    FR.full(tb)
                for fc in range(NFC):
                    if tb + 1 < NBLK and fc in (8, 12):
                        FR.ld(tb + 1, (fc - 8) // 4)
                        FR.st(tb + 1, (fc - 8) // 4)
                    if tb + 1 < NBLK and fc == 17:
                        FR.xsop(tb + 1, 0)
                    k = fc % 2
                    pg, pvb = (2, 3, 6)[fc % 3], (4, 5, 7)[fc % 3]

                    def f_up(e, fc=fc, pg=pg, pvb=pvb):
                        for kc in range(8):
                            e.matmul(ps[pg][:, :], lhsT=wup[:, kc, fc * 128:(fc + 1) * 128], rhs=hT[:, kc, :],
                                     start=(kc == 0), stop=(kc == 7))
                        for kc in range(8):
                            i = e.matmul(ps[pvb][:, :], lhsT=wup[:, kc, DFF + fc * 128:DFF + (fc + 1) * 128],
                                         rhs=hT[:, kc, :], start=(kc == 0), stop=(kc == 7))
                        return i
                    S.op("pe", f_up, reads=[B_wup] + B_hT, writes=[PB[pg], PB[pvb]])
                    if lb == 0:
                        S.op("act", lambda e, k=k: e.activation(out=gsb[k][:, 0:2], in_=gsb[k][:, 0:2], func=AF.Copy, scale=0.0),
                             writes=[B_gsb[k]])
                    else:
                        S.op("act", lambda e, k=k, fc=fc: e.copy(out=gsb[k][:, 0:2], in_=carry[:, fc, :]),
                             reads=[B_carry[fc]], writes=[B_gsb[k]])
                    S.op("act", lambda e, k=k, pg=pg: e.copy(out=gsb[k][:, 2:TB + 2], in_=ps[pg][:, :]),
                         writes=[B_gsb[k], PB[pg]])
                    S.op("act", lambda e, fc=fc, pg=pg: e.copy(out=carry[:, fc, :], in_=ps[pg][:, TB - 2:TB]),
                         writes=[B_carry[fc], PB[pg]])
                    cw = PV_CW + fc * 3
                    S.op("act", lambda e, k=k, fc=fc, cw=cw, pg=pg: e.activation(
                        out=ct1[k], in_=ps[pg][:, :], func=AF.Identity, scale=pvec[:, cw + 2:cw + 3],
                        bias=pvec[:, PV_CB + fc:PV_CB + fc + 1]),
                        reads=[B_const], writes=[B_ct1[k], PB[pg]])
                    S.op("dve", lambda e, k=k, cw=cw: e.scalar_tensor_tensor(
                        out=ct2[k], in0=gsb[k][:, 1:TB + 1], scalar=pvec[:, cw + 1:cw + 2], in1=ct1[k],
                        op0=ALU.mult, op1=ALU.add), reads=[B_gsb[k], B_ct1[k], B_const], writes=[B_ct2[k]])
                    S.op("dve", lambda e, k=k, cw=cw: e.scalar_tensor_tensor(
                        out=ct1[k], in0=gsb[k][:, 0:TB], scalar=pvec[:, cw:cw + 1], in1=ct2[k],
                        op0=ALU.mult, op1=ALU.add), reads=[B_gsb[k], B_ct2[k], B_const], writes=[B_ct1[k]])
                    def tail(fc=fc, k=k, pvb=pvb):
                        S.op("act", lambda e, k=k: e.activation(out=ct2[k], in_=ct1[k], func=AF.Silu),
                             reads=[B_ct1[k]], writes=[B_ct2[k]])
                        S.op("dve", lambda e, k=k, fc=fc, pvb=pvb: e.tensor_tensor(out=actT[:, fc, :], in0=ct2[k],
                                                                                  in1=ps[pvb][:, :], op=ALU.mult),
                             reads=[B_ct2[k]], writes=[B_actT, PB[pvb]])
                    if pend_tail[0] is not None:
                        pend_tail[0]()
                    pend_tail[0] = tail
                pend_tail[0]()
                pend_tail[0] = None
                for t in range(4):
                    if tb + 1 < NBLK:
                        FR.tr(tb + 1, t)
                        if t % 2 == 1:
                            FR.evac(tb + 1, t // 2)
                        if t < 2:
                            FR.ld(tb + 1, t + 2)
                            FR.st(tb + 1, t + 2)
                        if t < 3:
                            FR.xsop(tb + 1, t + 1)
                    k = cnt2 % 2
                    cnt2 += 1
                    r0 = tb * TB + t * 128
                    dma("pool", xr[k], srcB[r0:r0 + 128, :], f"xr{k}", writes=[B_xr[k]])
                    for nb in range(2):
                        pb = 6 + nb

                        def f_dn(e, t=t, nb=nb, pb=pb):
                            for fc in range(NFC):
                                i = e.matmul(ps[pb][:, :], lhsT=actT[:, fc, t * 128:(t + 1) * 128],
                                             rhs=wdn[:, fc, nb * 512:(nb + 1) * 512], start=(fc == 0), stop=(fc == NFC - 1))
                            return i
                        S.op("pe", f_dn, reads=[B_actT, B_wdn], writes=[PB[pb]])
                        S.op("dve", lambda e, nb=nb, pb=pb: e.tensor_tensor(
                            out=ot1[:, nb * 512:(nb + 1) * 512], in0=ps[pb][:, :], in1=Gf[:, nb * 512:(nb + 1) * 512],
                            op=ALU.mult), reads=[B_Gf], writes=[B_ot1, PB[pb]])
                        S.op("dve", lambda e, nb=nb, k=k: e.tensor_tensor(
                            out=xr[k][:, nb * 512:(nb + 1) * 512], in0=ot1[:, nb * 512:(nb + 1) * 512],
                            in1=xr[k][:, nb * 512:(nb + 1) * 512], op=ALU.add), reads=[B_ot1], writes=[B_xr[k]])
                    junkb = ct1[0].bitcast(BF16)
                    S.op("act", lambda e, k=k: e.activation(out=junkb, in_=xr[k], func=AF.Square,
                                                            accum_out=st2[:, k:k + 1]),
                         reads=[B_xr[k]], writes=[B_ct1[0], B_st2[k]])
                    S.op("act", lambda e, k=k: e.activation(out=st2[:, 4 + k:5 + k], in_=st2[:, k:k + 1], func=AF.Ln,
                                                            scale=1.0 / D, bias=pvec[:, 127:128]),
                         reads=[B_st2[k], B_const], writes=[B_st2[k]])
                    S.op("act", lambda e, k=k: e.activation(out=st2[:, k:k + 1], in_=st2[:, 4 + k:5 + k], func=AF.Exp,
                                                            scale=-0.5), reads=[B_st2[k]], writes=[B_st2[k]])
                    S.op("dve", lambda e, k=k: e.scalar_tensor_tensor(out=xr[k], in0=xr[k], scalar=st2[:, k:k + 1],
                                                                      in1=Gfin, op0=ALU.mult, op1=ALU.mult),
                         reads=[B_st2[k], B_const], writes=[B_xr[k]])
                    dma("sp", y_d[r0:r0 + 128, :], xr[k], f"xo{k}", reads=[B_xr[k]])
            S.fence()

        if "S" in phases:
            phase_S()
        if "A" in phases:
            phase_A()
        if "B" in phases:
            phase_B()
        S.emit()
    return nc


def _fm(v, n):
    return np.ascontiguousarray(np.asarray(v, np.float32).reshape(n, 128).T)


def prep_shared(inp):
    f = lambda a: np.ascontiguousarray(np.asarray(a, np.float32))
    sh = {}
    sh["w_ada"] = f(inp["w_ada"][0])
    sh["w_in"] = f(inp["w_in"][0])
    sh["w_out"] = f(inp["w_out"][0])
    sh["w_up"] = f(inp["w_up"][0])
    sh["w_down"] = f(inp["w_down"][0])
    sh["b_ada2"] = np.ascontiguousarray(np.broadcast_to(f(inp["b_ada"][0])[None, :], (NB, 6 * D)))
    pv = np.zeros((128, 128), np.float32)
    pv[:, 0:8] = _fm(inp["g_mix"][0], 8)
    pv[:, 8:16] = _fm(inp["g_ffn"][0], 8)
    pv[:, 16:20] = _fm(inp["g_attn_out"][0], 4)
    pv[:, 20:24] = _fm(inp["g_ssm_out"][0], 4)
    cw = f(inp["conv_w"][0])
    pv[:, 24:90] = np.ascontiguousarray(cw.reshape(3, NFC, 128).transpose(2, 1, 0)).reshape(128, NFC * 3)
    pv[:, 90:112] = _fm(inp["conv_b"][0], NFC)
    pv[:, 112:116] = _fm(f(inp["d_skip"][0]).reshape(-1), 4)
    pv[:, 116:120] = _fm(f(inp["b_glu"][0]).reshape(-1), 4)
    pv[:, 127] = EPS
    sh["pvec"] = pv
    sh["g_final"] = f(inp["g_final"]).reshape(1, D)
    sh["b_fgate"] = f(inp["b_fgate"][0]).reshape(8, 1)
    sel = np.zeros((NB, NB * 128), np.float32)
    for b in range(NB):
        sel[b, b * 128:(b + 1) * 128] = 1.0
    sh["sel"] = sel
    def pl(a):
        a = f(a)
        rest = a.shape[2:]
        a = a.reshape((16, 2, 64) + rest)
        a = np.moveaxis(a, 0, 2)
        return np.ascontiguousarray(a.reshape((128, 16) + rest))
    are = pl(inp["a_re"][0])
    aim = pl(inp["a_im"][0])
    ldt = pl(np.broadcast_to(f(inp["log_dt"][0])[:, None], (32, 64)))
    sh["ssm16"] = np.ascontiguousarray(np.concatenate([are, aim, ldt], axis=1))
    bre = pl(inp["ssm_b_re"][0]).reshape(128, 256)
    bim = pl(inp["ssm_b_im"][0]).reshape(128, 256)
    sh["ssmB"] = np.ascontiguousarray(np.concatenate([bre, bim], axis=1))
    cre = pl(np.transpose(f(inp["ssm_c_re"][0]), (0, 2, 1))).reshape(128, 256)
    cim = pl(np.transpose(f(inp["ssm_c_im"][0]), (0, 2, 1))).reshape(128, 256)
    sh["ssmC"] = np.ascontiguousarray(np.concatenate([cre, cim], axis=1))
    wg = f(inp["w_glu"][0])
    blk = np.zeros((128, 4, 128), np.float32)
    for g in range(32):
        q, g8 = divmod(g, 8)
        blk[g8 * 16:(g8 + 1) * 16, q, g8 * 16:(g8 + 1) * 16] = wg[g]
    sh["wglu_blk"] = blk.reshape(128, 512)
    return sh


_NC_CACHE = {}


def kernel(**inp):
    if "full" not in _NC_CACHE:
        _NC_CACHE["full"] = build("SAB")
    nc = _NC_CACHE["full"]
    sh = prep_shared(inp)
    x = np.asarray(inp["x"], np.float32)
    c = np.asarray(inp["c"], np.float32)
    in_maps = []
    for i in range(8):
        m = dict(sh)
        m["x"] = np.ascontiguousarray(x[NB * i:NB * (i + 1)].reshape(NTOK, D))
        m["c"] = np.ascontiguousarray(c[NB * i:NB * (i + 1)])
        in_maps.append(m)
    res = run_bass_kernel_spmd(nc, in_maps, core_ids=list(range(8)))
    out = np.concatenate([res.results[i]["y"].reshape(NB, SEQ, D) for i in range(8)], axis=0)
    return out.astype(np.float32)
```
